# Optimizing a Trainium2 kernel written in Bass

```python
import jax, jax.numpy as jnp
from jax import lax
import numpy as np

D_MODEL = 2048
BATCH = 2
SEQ = 16384
DEPTH = 2

N_MIXERS = 2
POOL_WINDOWS = (2, 4, 8, 16)
N_POOL_GROUPS = len(POOL_WINDOWS)
POOL_GROUP = D_MODEL // N_POOL_GROUPS
CONV_WIDTH = 3
D_FF = ((8 * D_MODEL // 3 + 255) // 256) * 256
N_A = (DEPTH + 1) // 2
N_B = DEPTH // 2
EPS = 1e-6

kernel_name = "hybrid_pool_shortconv_convffn_adaln"


def rmsnorm(x, g):
    xf = x.astype(jnp.float32)
    y = xf * lax.rsqrt(jnp.mean(xf * xf, axis=-1, keepdims=True) + EPS)
    return (y * g.astype(jnp.float32)).astype(x.dtype)


def modulate(h, shift, scale):
    return h * (1 + scale[:, None, :]) + shift[:, None, :]


def causal_dwconv(u, w, b):
    k_width = w.shape[0]
    s = u.shape[1]
    up = jnp.pad(u, ((0, 0), (k_width - 1, 0), (0, 0)))
    y = up[:, 0:s] * w[0]
    for k in range(1, k_width):
        y = y + up[:, k:k + s] * w[k]
    return y + b


def causal_multiscale_pool(h):
    bsz, s, _ = h.shape
    pos = jnp.arange(s)
    outs = []
    for gi, win in enumerate(POOL_WINDOWS):
        hg = h[:, :, gi * POOL_GROUP:(gi + 1) * POOL_GROUP].astype(jnp.float32)
        cs = jnp.cumsum(hg, axis=1)
        prev = jnp.pad(cs, ((0, 0), (win, 0), (0, 0)))[:, :s]
        count = jnp.minimum(pos + 1, win).astype(jnp.float32)
        mean = (cs - prev) / count[None, :, None]
        outs.append((mean - hg).astype(h.dtype))
    return jnp.stack(outs, axis=2)


def pool_mixer(h, pool_w, pool_scale):
    bsz, s, d = h.shape
    p = causal_multiscale_pool(h)
    y = jnp.einsum("bsgc,gce->bsge", p, pool_w).reshape(bsz, s, d)
    return y * pool_scale


def short_conv_mixer(h, bcx_w, conv_w, conv_b, out_w):
    z = h @ bcx_w
    bg, cg, u = jnp.split(z, 3, axis=-1)
    v = causal_dwconv(cg * u, conv_w, conv_b)
    return (bg * v) @ out_w


def conv_ffn(h, up_w, conv_w, conv_b, down_w):
    a = causal_dwconv(h @ up_w, conv_w, conv_b)
    g, v = jnp.split(a, 2, axis=-1)
    return (jax.nn.silu(g) * v) @ down_w


def setup_inputs(seed: int = 0) -> dict:
    key = jax.random.key(seed)
    ks = jax.random.split(key, 20)
    d, f = D_MODEL, D_FF
    n = lambda k, shape, s: jax.random.normal(k, shape, jnp.float32) * s
    return {
        "x": n(ks[0], (BATCH, SEQ, d), 1.0),
        "c": n(ks[1], (BATCH, d), 1.0),
        "ada_w": n(ks[2], (DEPTH, d, 6 * d), 0.1 * d ** -0.5),
        "ada_b": n(ks[3], (DEPTH, 6 * d), 0.02),
        "norm1_g": 1.0 + n(ks[4], (DEPTH, d), 0.02),
        "norm2_g": 1.0 + n(ks[5], (DEPTH, d), 0.02),
        "pool_w": n(ks[6], (N_A, N_POOL_GROUPS, POOL_GROUP, POOL_GROUP), POOL_GROUP ** -0.5),
        "pool_scale": 1.0 + n(ks[7], (N_A, d), 0.02),
        "bcx_w": n(ks[8], (N_B, d, 3 * d), d ** -0.5),
        "sconv_w": n(ks[9], (N_B, CONV_WIDTH, d), CONV_WIDTH ** -0.5),
        "sconv_b": n(ks[10], (N_B, d), 0.02),
        "sout_w": n(ks[11], (N_B, d, d), d ** -0.5),
        "up_w": n(ks[12], (DEPTH, d, 2 * f), d ** -0.5),
        "fconv_w": n(ks[13], (DEPTH, CONV_WIDTH, 2 * f), CONV_WIDTH ** -0.5),
        "fconv_b": n(ks[14], (DEPTH, 2 * f), 0.02),
        "down_w": n(ks[15], (DEPTH, f, d), f ** -0.5),
        "final_g": 1.0 + n(ks[16], (d,), 0.02),
    }


def reference(x, c, ada_w, ada_b, norm1_g, norm2_g, pool_w, pool_scale, bcx_w, sconv_w,
              sconv_b, sout_w, up_w, fconv_w, fconv_b, down_w, final_g):
    for i in range(DEPTH):
        mod = c @ ada_w[i] + ada_b[i]
        sh1, sc1, g1, sh2, sc2, g2 = jnp.split(mod, 6, axis=-1)
        h = modulate(rmsnorm(x, norm1_g[i]), sh1, sc1)
        j = i // N_MIXERS
        if i % N_MIXERS == 0:
            y = pool_mixer(h, pool_w[j], pool_scale[j])
        else:
            y = short_conv_mixer(h, bcx_w[j], sconv_w[j], sconv_b[j], sout_w[j])
        x = x + (1 + g1)[:, None, :] * y
        h = modulate(rmsnorm(x, norm2_g[i]), sh2, sc2)
        y = conv_ffn(h, up_w[i], fconv_w[i], fconv_b[i], down_w[i])
        x = x + (1 + g2)[:, None, :] * y
    return rmsnorm(x, final_g)
```

```python
import numpy as np
import concourse.bass as bass
import concourse.mybir as mybir
from concourse.bass_utils import run_bass_kernel_spmd

F32 = mybir.dt.float32
BF16 = mybir.dt.bfloat16
AF = mybir.ActivationFunctionType
ALU = mybir.AluOpType

D = 2048
KC = 16
FF = 5632
FC = 44
HALO = 23
WV = 1024
T = WV + HALO
NBLK = 3
BW = T // NBLK
EPS = 1e-6
NS = 4
NTMP = 5
POOL_WINDOWS = (2, 4, 8, 16)
FFB = [8, 8, 8, 8, 8, 4]

VOFF = {}
_o = 0
for _n, _s in [("adab", 192), ("n1g", 32), ("n2g", 32), ("fing", 16), ("pscale", 16), ("scw", 48),
               ("scb", 16), ("fcw", 528), ("fcb", 176), ("c", 16), ("mask", 32), ("corr", 64)]:
    VOFF[_n] = _o
    _o += _s
NV = _o


class _Eng:
    def __init__(self, name):
        self.name = name
        self.ops = []
        self.count = 0
        self.sem = None
        self.waited = {}


class Gen:
    def __init__(self, nc, NT):
        self.nc = nc
        self.NT = NT
        self.eng = {n: _Eng(n) for n in ("pe", "act", "dve", "pool", "sp")}
        self.last_w = {}
        self.readers = {}
        self.dma_cnt = {}
        self.gcount = 0
        self.tmp_ctr = 0
        self.slot_ctr = 0
        self.bg_todo = []
        self.bg_pending = None
        self.bg_cast = None

    def op(self, eng, build, reads=(), writes=(), signal=True, dma_sem=None):
        E = self.eng[eng]
        is_dma = dma_sem is not None
        waits = {}

        def need(tok, raw):
            if tok is None:
                return
            sem, val, kind = tok
            if kind != "dma" and not is_dma and kind == eng and eng == "pe":
                return
            if kind != "dma":
                assert val <= self.eng[kind].count, ("deferred token not yet signalled", eng, kind)
            if E.waited.get(id(sem), 0) >= val:
                return
            if waits.get(id(sem), (None, 0))[1] < val:
                waits[id(sem)] = (sem, val)

        for k in reads:
            need(self.last_w.get(k), True)
        for k in writes:
            need(self.last_w.get(k), False)
            for tok in self.readers.get(k, {}).values():
                need(tok, False)
        wl = list(waits.values())
        for sem, val in wl:
            E.waited[id(sem)] = val
        if is_dma:
            self.dma_cnt[id(dma_sem)] = self.dma_cnt.get(id(dma_sem), 0) + 16
            tok = (dma_sem, self.dma_cnt[id(dma_sem)], "dma")
            inc = (dma_sem, 16)
        elif signal:
            E.count += 1
            tok = (E.sem, E.count, eng)
            inc = (E.sem, 1)
        else:
            tok = (E.sem, E.count + 1, eng)
            inc = None
        for k in reads:
            r = self.readers.setdefault(k, {})
            r[id(tok[0])] = tok
        for k in writes:
            self.last_w[k] = tok
            self.readers[k] = {}
        E.ops.append((wl, build, inc))
        return tok

    def replay(self, eng, e):
        for wl, build, inc in self.eng[eng].ops:
            for sem, val in wl:
                e.wait_ge(sem, val)
            ins = build(e)
            if inc is not None:
                ins.then_inc(inc[0], inc[1])

    def new_tmp(self):
        s = self.tmp_ctr % NTMP
        self.tmp_ctr += 1
        return s

    def tmp(self, s):
        return self.TMP[:, s, :]

    def load_slot(self, src, kk, nn):
        s = self.slot_ctr % NS
        self.slot_ctr += 1
        dst = self.WR[:, s, 0:kk * nn].rearrange("p (k n) -> p k n", n=nn)
        srcv = src.rearrange("(k p) n -> p k n", p=128)
        self.op("pool", lambda e: e.dma_start(out=dst, in_=srcv), writes=[("W", s)], dma_sem=self.SW[s])
        return s, dst

    def pe_group(self, ksteps):
        st = self.gcount % 2
        self.gcount += 1
        n = len(ksteps)
        for i, (lhsT, rhs_fn, rk) in enumerate(ksteps):
            first, last = (i == 0), (i == n - 1)

            def build(e, lhsT=lhsT, rhs_fn=rhs_fn, first=first, last=last, st=st):
                ins = None
                for b in range(NBLK):
                    bank = st * 3 + b
                    ins = e.matmul(self.PS[:, bank * 512:bank * 512 + BW], lhsT, rhs_fn(b),
                                   start=first, stop=last)
                return ins
            self.op("pe", build, reads=rk, writes=[("PS", st)], signal=last)
        view = self.PS[:, st * 1536:(st + 1) * 1536].rearrange("p (b c) -> p b c", c=512)[:, :, 0:BW]
        return st, view

    def evac(self, st, view, scale=None, extra_reads=()):
        s = self.new_tmp()
        out = self.TMP[:, s, :].rearrange("p (b c) -> p b c", c=BW)
        if scale is None:
            self.op("act", lambda e: e.activation(out=out, in_=view, func=AF.Identity),
                    reads=[("PS", st)], writes=[("TMP", s)])
        else:
            self.op("act", lambda e: e.activation(out=out, in_=view, func=AF.Identity, scale=scale),
                    reads=[("PS", st)] + list(extra_reads), writes=[("TMP", s)])
        return s

    def blkfn(self, ap2d):
        return lambda b: ap2d[:, b * BW:(b + 1) * BW]

    def vec(self, name, idx=0, n=1):
        o = VOFF[name] + idx
        return self.VEC[:, o:o + n]

    def ada_load(self, l, j):
        return self.load_slot(self.ada_w[l, :, 256 * j:256 * j + 256], 16, 256)

    def ada_mm(self, l, j, s, dst):
        for mm in range(2):
            m = 2 * j + mm
            part_bank = 6 if (m < 32 or 48 <= m < 80) else 7
            col = m if m < 32 else (m - 32 if m < 48 else (m - 48 + 32 if m < 80 else m - 80 + 16))
            for k in range(16):
                def build(e, k=k, mm=mm, dst=dst, bank=part_bank, col=col):
                    return e.matmul(self.PS[:, bank * 512 + col:bank * 512 + col + 1],
                                    dst[:, k, mm * 128:(mm + 1) * 128], self.CB[:, k:k + 1],
                                    start=(k == 0), stop=(k == 15))
                self.op("pe", build, reads=[("W", s), ("CB",)], writes=[("PS", part_bank)], signal=(k == 15 and mm == 1))

    def ada_block(self, l, j):
        s, dst = self.ada_load(l, j)
        self.ada_mm(l, j, s, dst)

    def bg_step(self):
        if self.bg_cast is not None:
            l, m = self.bg_cast
            self.bg_cast = None
            part_bank = 6 if (m < 32 or 48 <= m < 80) else 7
            col = m if m < 32 else (m - 32 if m < 48 else (m - 48 + 32 if m < 80 else m - 80 + 16))
            for k in range(16):
                def build(e, k=k, bank=part_bank, col=col):
                    return e.matmul(self.PS[:, bank * 512 + col:bank * 512 + col + 1],
                                    self.ADAB[:, k, :], self.CB[:, k:k + 1], start=(k == 0), stop=(k == 15))
                self.op("pe", build, reads=[("ADAB",), ("CB",)], writes=[("PS", part_bank)], signal=(k == 15))
        if self.bg_pending is not None:
            self.bg_cast = self.bg_pending
            self.bg_pending = None
            self.op("dve", lambda e: e.tensor_copy(out=self.ADAB[:, :, :], in_=self.ADAF[:, :, :]),
                    reads=[("ADAF",)], writes=[("ADAB",)])
        if self.bg_todo:
            l, m = self.bg_todo.pop(0)
            srcv = self.ada_w[l, :, 128 * m:128 * m + 128].rearrange("(k p) n -> p k n", p=128)
            self.op("sp", lambda e: e.dma_start(out=self.ADAF[:, :, :], in_=srcv), writes=[("ADAF",)], dma_sem=self.SA)
            self.bg_pending = (l, m)

    def ada_part(self, l, part):
        m0, m1, bank, c0 = {"a": (0, 32, 6, 0), "b": (32, 48, 7, 0), "c": (48, 80, 6, 32), "d": (80, 96, 7, 16)}[part]
        n = m1 - m0
        raw = self.MODRAW[:, m0:m1]
        mod = self.MOD[:, l, m0:m1]
        self.op("act", lambda e: e.activation(out=raw, in_=self.PS[:, bank * 512 + c0:bank * 512 + c0 + n],
                                              func=AF.Identity),
                reads=[("PS", bank)], writes=[("MODRAW", part)])
        bia = self.vec("adab", l * 96 + m0, n)
        self.op("dve", lambda e: e.tensor_tensor(out=mod, in0=raw, in1=bia, op=ALU.add),
                reads=[("MODRAW", part), ("VEC",)], writes=[("MOD", l, part)])
        DER = self.DER
        if part == "a":
            gn = self.vec("n1g", l * 16, 16)
            self.op("dve", lambda e: e.scalar_tensor_tensor(out=DER[:, l, 0, :], in0=self.MOD[:, l, 16:32], scalar=1.0,
                                                            in1=gn, op0=ALU.add, op1=ALU.mult),
                    reads=[("MOD", l, part), ("VEC",)], writes=[("DER", l, 0)])
        elif part == "b":
            if l == 0:
                ps = self.vec("pscale", 0, 16)
                self.op("dve", lambda e: e.scalar_tensor_tensor(out=DER[:, l, 2, :], in0=self.MOD[:, l, 32:48], scalar=1.0,
                                                                in1=ps, op0=ALU.add, op1=ALU.mult),
                        reads=[("MOD", l, part), ("VEC",)], writes=[("DER", l, 2)])
            else:
                self.op("dve", lambda e: e.tensor_scalar(out=DER[:, l, 2, :], in0=self.MOD[:, l, 32:48], scalar1=1.0,
                                                         scalar2=None, op0=ALU.add),
                        reads=[("MOD", l, part)], writes=[("DER", l, 2)])
        elif part == "c":
            gn = self.vec("n2g", l * 16, 16)
            self.op("dve", lambda e: e.scalar_tensor_tensor(out=DER[:, l, 3, :], in0=self.MOD[:, l, 64:80], scalar=1.0,
                                                            in1=gn, op0=ALU.add, op1=ALU.mult),
                    reads=[("MOD", l, part), ("VEC",)], writes=[("DER", l, 3)])
        else:
            self.op("dve", lambda e: e.tensor_scalar(out=DER[:, l, 5, :], in0=self.MOD[:, l, 80:96], scalar1=1.0,
                                                     scalar2=None, op0=ALU.add),
                    reads=[("MOD", l, part)], writes=[("DER", l, 5)])

    def coef(self, l, which):
        if which == "A1":
            return self.DER[:, l, 0, :], [("DER", l, 0)]
        if which == "B1":
            return self.MOD[:, l, 0:16], [("MOD", l, "a")]
        if which == "G1":
            return self.DER[:, l, 2, :], [("DER", l, 2)]
        if which == "A2":
            return self.DER[:, l, 3, :], [("DER", l, 3)]
        if which == "B2":
            return self.MOD[:, l, 48:64], [("MOD", l, "c")]
        if which == "G2":
            return self.DER[:, l, 5, :], [("DER", l, 5)]
        raise KeyError(which)

    def norm_stats(self):
        X, H = self.X, self.H
        for k in range(KC):
            if k % 2 == 0:
                self.op("act", lambda e, k=k: e.activation(out=H[:, k, :], in_=X[:, k, :], func=AF.Square),
                        reads=[("X", k)], writes=[("H", k)])
            else:
                self.op("dve", lambda e, k=k: e.tensor_tensor(out=H[:, k, :], in0=X[:, k, :], in1=X[:, k, :], op=ALU.mult),
                        reads=[("X", k)], writes=[("H", k)])
        steps = [(self.ONES[:, :], self.blkfn(H[:, k, :]), [("H", k), ("ONES",)]) for k in range(KC)]
        st, view = self.pe_group(steps)
        s = self.new_tmp()
        ts = self.tmp(s)
        out3 = self.TMP[:, s, :].rearrange("p (b c) -> p b c", c=BW)
        self.op("act", lambda e: e.activation(out=out3, in_=view, func=AF.Sqrt, bias=self.EPSV[:, 0:1]),
                reads=[("PS", st), ("EPSV",)], writes=[("TMP", s)])
        self.op("dve", lambda e: e.reciprocal(out=self.RSTD[:, :], in_=ts),
                reads=[("TMP", s)], writes=[("RSTD",)])

    def modulate(self, l, site, first_tile, to_h=True):
        A, ak = self.coef(l, "A%d" % site)
        Bv, bk = self.coef(l, "B%d" % site)
        X, H = self.X, self.H
        for k in range(KC):
            s = self.new_tmp()
            ts = self.tmp(s)
            self.op("dve", lambda e, k=k, ts=ts: e.tensor_tensor(out=ts, in0=X[:, k, :], in1=self.RSTD[:, :], op=ALU.mult),
                    reads=[("X", k), ("RSTD",)], writes=[("TMP", s)])
            self.op("act", lambda e, k=k, ts=ts: e.activation(out=H[:, k, :], in_=ts, func=AF.Identity,
                                                              bias=Bv[:, k:k + 1], scale=A[:, k:k + 1]),
                    reads=[("TMP", s)] + ak + bk, writes=[("H", k)])
            if first_tile:
                self.op("dve", lambda e, k=k: e.tensor_tensor(out=H[:, k, 0:HALO], in0=H[:, k, 0:HALO],
                                                              in1=self.MASKB[:, :], op=ALU.mult),
                        reads=[("H", k), ("MASKB",)], writes=[("H", k)])

    def resid_evac(self, st, view, l, which, m):
        G, gk = self.coef(l, which)
        s = self.evac(st, view, scale=G[:, m:m + 1], extra_reads=gk)
        ts = self.tmp(s)
        X = self.X
        self.op("dve", lambda e: e.tensor_tensor(out=X[:, m, :], in0=X[:, m, :], in1=ts, op=ALU.add),
                reads=[("TMP", s), ("X", m)], writes=[("X", m)])

    def conv3(self, src_s, w_ap3, b_ap):
        src = self.tmp(src_s)
        d = self.new_tmp()
        dst = self.tmp(d)
        self.op("dve", lambda e: e.tensor_scalar(out=dst, in0=src, scalar1=w_ap3[2], scalar2=b_ap,
                                                 op0=ALU.mult, op1=ALU.add),
                reads=[("TMP", src_s), ("VEC",)], writes=[("TMP", d)])
        self.op("dve", lambda e: e.scalar_tensor_tensor(out=dst[:, 1:T], in0=src[:, 0:T - 1], scalar=w_ap3[1],
                                                        in1=dst[:, 1:T], op0=ALU.mult, op1=ALU.add),
                reads=[("TMP", src_s), ("TMP", d), ("VEC",)], writes=[("TMP", d)])
        self.op("dve", lambda e: e.scalar_tensor_tensor(out=dst[:, 2:T], in0=src[:, 0:T - 2], scalar=w_ap3[0],
                                                        in1=dst[:, 2:T], op0=ALU.mult, op1=ALU.add),
                reads=[("TMP", src_s), ("TMP", d), ("VEC",)], writes=[("TMP", d)])
        return d

    def tref(self, ref):
        kind, u = ref
        if kind == "TMP":
            return self.TMP[:, u, :], [("TMP", u)]
        return self.PT[:, u, :], [("MID", (2 * u) // 8, (2 * u) % 8), ("MID", (2 * u + 1) // 8, (2 * u + 1) % 8)]

    def pool_front(self, first_tile):
        l = 0
        X, H = self.X, self.H
        A, ak = self.coef(l, "A1")
        Bv, bk = self.coef(l, "B1")
        self.norm_stats()
        dsets = [[("PT", 0), ("PT", 1), ("PT", 2)], [("PT", 3), ("PT", 4), ("PT", 5)]]
        psets = [[("PT", 6), ("PT", 7), ("TMP", 0)], [("TMP", 1), ("TMP", 2), ("TMP", 3)]]

        def stage1(k, ve, st):
            t, tk = self.tref(st[0])
            self.op(ve, lambda e: e.tensor_tensor(out=t, in0=X[:, k, :], in1=self.RSTD[:, :], op=ALU.mult),
                    reads=[("X", k), ("RSTD",)], writes=tk)
            self.op("act", lambda e: e.activation(out=t, in_=t, func=AF.Identity, bias=Bv[:, k:k + 1], scale=A[:, k:k + 1]),
                    reads=tk + ak + bk, writes=tk)
            if first_tile:
                self.op(ve, lambda e: e.tensor_tensor(out=t[:, 0:HALO], in0=t[:, 0:HALO], in1=self.vec("mask", 0, HALO),
                                                      op=ALU.mult), reads=tk + [("VEC",)], writes=tk)

        def stage2(k, ve, st):
            g = k // 4
            win = POOL_WINDOWS[g]
            cur, ck = self.tref(st[0])
            dd, i = 1, 0
            while dd < win:
                nx, nk = self.tref(st[1 + (i % 2)])
                self.op(ve, lambda e, cur=cur, nx=nx, dd=dd: e.tensor_tensor(out=nx[:, dd:T], in0=cur[:, dd:T],
                                                                              in1=cur[:, 0:T - dd], op=ALU.add),
                        reads=ck, writes=nk)
                cur, ck = nx, nk
                dd *= 2
                i += 1
            if first_tile:
                self.op(ve, lambda e, cur=cur: e.tensor_tensor(out=cur[:, HALO:HALO + 16], in0=cur[:, HALO:HALO + 16],
                                                               in1=self.vec("corr", g * 16, 16), op=ALU.mult),
                        reads=ck + [("VEC",)], writes=ck)
            return cur, ck

        def final(k, st, cur, ck):
            win = POOL_WINDOWS[k // 4]
            t, tk = self.tref(st[0])
            self.op("dve", lambda e: e.scalar_tensor_tensor(out=H[:, k, :], in0=cur, scalar=1.0 / win, in1=t,
                                                            op0=ALU.mult, op1=ALU.subtract),
                    reads=ck + tk, writes=[("H", k)])

        for g in range(4):
            c = [4 * g, 4 * g + 1, 4 * g + 2, 4 * g + 3]
            if g == 0:
                stage1(c[0], "dve", dsets[0])
                stage1(c[1], "dve", dsets[1])
            stage1(c[2], "dve", psets[0])
            stage1(c[3], "dve", psets[1])
            r0 = stage2(c[0], "dve", dsets[0])
            final(c[0], dsets[0], *r0)
            r1 = stage2(c[1], "dve", dsets[1])
            final(c[1], dsets[1], *r1)
            if g < 3:
                stage1(c[0] + 4, "dve", dsets[0])
                stage1(c[1] + 4, "dve", dsets[1])
            r2 = stage2(c[2], "dve", psets[0])
            r3 = stage2(c[3], "dve", psets[1])
            final(c[2], psets[0], *r2)
            final(c[3], psets[1], *r3)

    def pool_w_load(self):
        out = []
        for gp in range(2):
            srcs = []
            s = self.slot_ctr % NS
            self.slot_ctr += 1
            for gi in range(2):
                g = 2 * gp + gi
                dst = self.WR[:, s, gi * 2048:(gi + 1) * 2048].rearrange("p (k n) -> p k n", n=512)
                srcv = self.pool_w[0, g].rearrange("(k p) n -> p k n", p=128)
                self.op("pool", lambda e, dst=dst, srcv=srcv: e.dma_start(out=dst, in_=srcv),
                        writes=[("W", s)], dma_sem=self.SW[s])
                srcs.append(dst)
            out.append((s, srcs))
        return out

    def pool_back(self, pre=None):
        H = self.H
        if pre is None:
            pre = self.pool_w_load()
        for gp in range(2):
            s, srcs = pre[gp]
            for gi in range(2):
                g = 2 * gp + gi
                wv = srcs[gi]
                for mc in range(4):
                    steps = [(wv[:, kc, mc * 128:(mc + 1) * 128], self.blkfn(H[:, 4 * g + kc, :]),
                              [("H", 4 * g + kc), ("W", s)]) for kc in range(4)]
                    st, view = self.pe_group(steps)
                    self.resid_evac(st, view, 0, "G1", 4 * g + mc)

    def ffn(self, l, first_tile):
        H, MID = self.H, self.MID
        self.norm_stats()
        self.modulate(l, 2, first_tile)
        upw, dnw = self.up_w, self.down_w

        def up_block(b):
            nb = FFB[b]
            j0 = sum(FFB[:b])
            for jj in range(0, nb, 2):
                j = j0 + jj
                sg, wg = self.load_slot(upw[l, :, 128 * j:128 * j + 256], 16, 256)
                sv, wv = self.load_slot(upw[l, :, FF + 128 * j:FF + 128 * j + 256], 16, 256)
                for e2 in range(2):
                    jc = j + e2
                    steps = [(wg[:, k, e2 * 128:(e2 + 1) * 128], self.blkfn(H[:, k, :]), [("H", k), ("W", sg)])
                             for k in range(KC)]
                    st, view = self.pe_group(steps)
                    u = self.evac(st, view)
                    w3 = [self.vec("fcw", l * 264 + tap * 88 + jc, 1) for tap in range(3)]
                    ag = self.conv3(u, w3, self.vec("fcb", l * 88 + jc, 1))
                    tg = self.tmp(ag)
                    self.op("act", lambda e, tg=tg: e.activation(out=tg, in_=tg, func=AF.Silu),
                            reads=[("TMP", ag)], writes=[("TMP", ag)])
                    steps = [(wv[:, k, e2 * 128:(e2 + 1) * 128], self.blkfn(H[:, k, :]), [("H", k), ("W", sv)])
                             for k in range(KC)]
                    st, view = self.pe_group(steps)
                    u = self.evac(st, view)
                    w3 = [self.vec("fcw", l * 264 + tap * 88 + FC + jc, 1) for tap in range(3)]
                    av = self.conv3(u, w3, self.vec("fcb", l * 88 + FC + jc, 1))
                    tv = self.tmp(av)
                    mo = MID[:, b % 2, jj + e2, :]
                    self.op("dve", lambda e, tg=tg, tv=tv, mo=mo: e.tensor_tensor(out=mo, in0=tg, in1=tv, op=ALU.mult),
                            reads=[("TMP", ag), ("TMP", av)], writes=[("MID", b % 2, jj + e2)])
                    self.bg_step()

        def down_block(b):
            nb = FFB[b]
            j0 = sum(FFB[:b])
            for q in range(4):
                s, wd = self.load_slot(dnw[l, 128 * j0:128 * (j0 + nb), 512 * q:512 * q + 512], nb, 512)
                for mc in range(4):
                    m = 4 * q + mc
                    steps = [(wd[:, kc, mc * 128:(mc + 1) * 128], self.blkfn(MID[:, b % 2, kc, :]),
                              [("MID", b % 2, kc), ("W", s)]) for kc in range(nb)]
                    st, view = self.pe_group(steps)
                    self.resid_evac(st, view, l, "G2", m)
                    if mc % 2 == 1:
                        self.bg_step()

        nbk = len(FFB)
        up_block(0)
        for b in range(1, nbk):
            up_block(b)
            down_block(b - 1)
        down_block(nbk - 1)

    def sconv_layer(self, first_tile):
        l = 1
        H, BV = self.H, self.BV
        self.norm_stats()
        self.modulate(l, 1, first_tile)
        bw = self.bcx_w
        for j in range(0, KC, 2):
            sc, wc = self.load_slot(bw[0, :, D + 128 * j:D + 128 * j + 256], 16, 256)
            su, wu = self.load_slot(bw[0, :, 2 * D + 128 * j:2 * D + 128 * j + 256], 16, 256)
            sb, wb = self.load_slot(bw[0, :, 128 * j:128 * j + 256], 16, 256)
            for e2 in range(2):
                jc = j + e2
                steps = [(wc[:, k, e2 * 128:(e2 + 1) * 128], self.blkfn(H[:, k, :]), [("H", k), ("W", sc)]) for k in range(KC)]
                st, view = self.pe_group(steps)
                u1 = self.evac(st, view)
                steps = [(wu[:, k, e2 * 128:(e2 + 1) * 128], self.blkfn(H[:, k, :]), [("H", k), ("W", su)]) for k in range(KC)]
                st, view = self.pe_group(steps)
                u2 = self.evac(st, view)
                t1, t2 = self.tmp(u1), self.tmp(u2)
                self.op("dve", lambda e, t1=t1, t2=t2: e.tensor_tensor(out=t1, in0=t1, in1=t2, op=ALU.mult),
                        reads=[("TMP", u1), ("TMP", u2)], writes=[("TMP", u1)])
                w3 = [self.vec("scw", tap * 16 + jc, 1) for tap in range(3)]
                v = self.conv3(u1, w3, self.vec("scb", jc, 1))
                tv = self.tmp(v)
                steps = [(wb[:, k, e2 * 128:(e2 + 1) * 128], self.blkfn(H[:, k, :]), [("H", k), ("W", sb)]) for k in range(KC)]
                st, view = self.pe_group(steps)
                u3 = self.evac(st, view)
                t3 = self.tmp(u3)
                bo = BV[:, jc, :]
                self.op("dve", lambda e, t3=t3, tv=tv, bo=bo: e.tensor_tensor(out=bo, in0=t3, in1=tv, op=ALU.mult),
                        reads=[("TMP", u3), ("TMP", v)], writes=[("MID", jc // 8, jc % 8)])
        for mp in range(0, KC, 2):
            s, ws = self.load_slot(self.sout_w[0, :, 128 * mp:128 * mp + 256], 16, 256)
            for e2 in range(2):
                m = mp + e2
                steps = [(ws[:, k, e2 * 128:(e2 + 1) * 128], self.blkfn(BV[:, k, :]),
                          [("MID", k // 8, k % 8), ("W", s)]) for k in range(KC)]
                st, view = self.pe_group(steps)
                self.resid_evac(st, view, 1, "G1", m)

    def final_norm(self, i):
        X = self.X
        self.norm_stats()
        for k in range(KC):
            s = self.new_tmp()
            ts = self.tmp(s)
            self.op("dve", lambda e, k=k, ts=ts: e.tensor_tensor(out=ts, in0=X[:, k, :], in1=self.RSTD[:, :], op=ALU.mult),
                    reads=[("X", k), ("RSTD",)], writes=[("TMP", s)])
            fg = self.vec("fing", k, 1)
            self.op("act", lambda e, ts=ts, fg=fg: e.activation(out=ts, in_=ts, func=AF.Identity, scale=fg),
                    reads=[("TMP", s), ("VEC",)], writes=[("TMP", s)])
            dst = self.outT[k * 128:(k + 1) * 128, i * WV:(i + 1) * WV]
            self.op("sp", lambda e, ts=ts, dst=dst: e.dma_start(out=dst, in_=ts[:, HALO:T]),
                    reads=[("TMP", s)], dma_sem=self.SO[s])

    def generate(self):
        self.op("sp", lambda e: e.dma_start(out=self.VEC[:, :], in_=self.vecs), writes=[("VEC",)], dma_sem=self.SV)
        self.op("dve", lambda e: e.memset(self.ONES[:, :], 1.0 / D), writes=[("ONES",)])
        self.op("dve", lambda e: e.memset(self.EPSV[:, :], EPS), writes=[("EPSV",)])
        self.op("dve", lambda e: e.memset(self.TMP[:, :, :], 0.0), writes=[("TMP", t) for t in range(NTMP)])
        self.op("dve", lambda e: e.memset(self.PT[:, :, :], 0.0), writes=[("MID", a, b) for a in range(2) for b in range(8)])
        self.op("dve", lambda e: e.tensor_copy(out=self.CB[:, :], in_=self.vec("c", 0, 16)), reads=[("VEC",)], writes=[("CB",)])
        self.op("dve", lambda e: e.tensor_copy(out=self.MASKB[:, :], in_=self.vec("mask", 0, HALO)),
                reads=[("VEC",)], writes=[("MASKB",)])
        for i in range(self.NT):
            first = (i == 0)
            for k in range(KC):
                self.op("sp", lambda e, i=i, k=k: e.dma_start(
                    out=self.X[:, k, :], in_=self.xT[k * 128:(k + 1) * 128, i * WV:i * WV + T]),
                    writes=[("X", k)], dma_sem=self.SXK[k])
            if first:
                for j in range(16):
                    self.ada_block(0, j)
                self.ada_part(0, "a")
            pre = None if first else self.pool_w_load()
            self.pool_front(first)
            if first:
                for j in range(16, 24):
                    self.ada_block(0, j)
                self.ada_part(0, "b")
                for j in range(24, 40):
                    self.ada_block(0, j)
                self.ada_part(0, "c")
                for j in range(40, 48):
                    self.ada_block(0, j)
                self.ada_part(0, "d")
                self.bg_todo = [(1, m) for m in range(96)]
            self.pool_back(pre)
            self.ffn(0, first)
            if first:
                while self.bg_todo or self.bg_pending is not None or self.bg_cast is not None:
                    self.bg_step()
                for part in "abcd":
                    self.ada_part(1, part)
            self.sconv_layer(first)
            self.ffn(1, first)
            self.final_norm(i)
        finals = [(self.SO[s], self.dma_cnt.get(id(self.SO[s]), 0)) for s in range(NTMP)]

        def fin(e):
            ins = None
            for sem, val in finals:
                if val > 0:
                    ins = e.wait_ge(sem, val)
            return ins
        self.eng["sp"].ops.append(([], fin, None))


def build_program(NT):
    CW = NT * WV
    nc = bass.Bass("TRN2", target_bir_lowering=False)
    g = Gen(nc, NT)
    g.xT = nc.dram_tensor("xT", [D, CW + HALO], F32, kind="ExternalInput").ap()
    g.vecs = nc.dram_tensor("vecs", [128, NV], F32, kind="ExternalInput").ap()
    g.ada_w = nc.dram_tensor("ada_w", [2, D, 6 * D], F32, kind="ExternalInput").ap()
    g.pool_w = nc.dram_tensor("pool_w", [1, 4, 512, 512], F32, kind="ExternalInput").ap()
    g.bcx_w = nc.dram_tensor("bcx_w", [1, D, 3 * D], F32, kind="ExternalInput").ap()
    g.sout_w = nc.dram_tensor("sout_w", [1, D, D], F32, kind="ExternalInput").ap()
    g.up_w = nc.dram_tensor("up_w", [2, D, 2 * FF], F32, kind="ExternalInput").ap()
    g.down_w = nc.dram_tensor("down_w", [2, FF, D], F32, kind="ExternalInput").ap()
    g.outT = nc.dram_tensor("outT", [D, CW], F32, kind="ExternalOutput").ap()
    from contextlib import ExitStack
    with ExitStack() as es:
        def sb(name, shape, dt):
            return es.enter_context(nc.sbuf_tensor(name, shape, dt))
        g.X = sb("X", [128, KC, T], F32)
        g.H = sb("H", [128, KC, T], BF16)
        g.MIDF = sb("MID", [128, 16 * T], BF16)
        g.MID = g.MIDF[:, :].rearrange("p (a b t) -> p a b t", a=2, b=8)
        g.BV = g.MIDF[:, :].rearrange("p (c t) -> p c t", t=T)
        g.PT = g.MIDF[:, :].bitcast(F32).rearrange("p (u t) -> p u t", t=T)
        g.WR = sb("WR", [128, NS, 4096], BF16)
        g.TMP = sb("TMP", [128, NTMP, T], F32)
        g.RSTD = sb("RSTD", [128, T], F32)
        g.VEC = sb("VEC", [128, NV], F32)
        g.MODRAW = sb("MODRAW", [128, 96], F32)
        g.MOD = sb("MOD", [128, 2, 96], F32)
        g.DER = sb("DER", [128, 2, 6, 16], F32)
        g.ONES = sb("ONES", [128, 128], BF16)
        g.CB = sb("CB", [128, 16], BF16)
        g.MASKB = sb("MASKB", [128, HALO], BF16)
        g.EPSV = sb("EPSV", [128, 1], F32)
        g.ADAF = sb("ADAF", [128, 16, 128], F32)
        g.ADAB = sb("ADAB", [128, 16, 128], BF16)
        g.PS = es.enter_context(nc.psum_tensor("PS", [128, 4096], F32))
        for n in g.eng:
            g.eng[n].sem = es.enter_context(nc.semaphore("prog_" + n))
        g.SW = [es.enter_context(nc.semaphore("sw%d" % s)) for s in range(NS)]
        g.SO = [es.enter_context(nc.semaphore("so%d" % s)) for s in range(NTMP)]
        g.SXK = [es.enter_context(nc.semaphore("sx%d" % k)) for k in range(KC)]
        g.SV = es.enter_context(nc.semaphore("sv"))
        g.SA = es.enter_context(nc.semaphore("sa"))
        g.generate()
        with nc.Block() as block:
            @block.tensor
            def _(e):
                g.replay("pe", e)

            @block.scalar
            def _(e):
                g.replay("act", e)

            @block.vector
            def _(e):
                g.replay("dve", e)

            @block.gpsimd
            def _(e):
                g.replay("pool", e)

            @block.sync
            def _(e):
                g.replay("sp", e)
    return nc


def pack_vecs(c_row, q, ada_b, norm1_g, norm2_g, final_g, pool_scale, sconv_w, sconv_b, fconv_w, fconv_b):
    v = np.zeros((128, NV), np.float32)

    def put(name, off, arr):
        o = VOFF[name] + off
        a = np.asarray(arr, np.float32).reshape(-1, 128).T
        v[:, o:o + a.shape[1]] = a
    for l in range(2):
        put("adab", l * 96, ada_b[l])
        put("n1g", l * 16, norm1_g[l])
        put("n2g", l * 16, norm2_g[l])
        for tap in range(3):
            put("fcw", l * 264 + tap * 88, fconv_w[l, tap])
        put("fcb", l * 88, fconv_b[l])
    put("fing", 0, final_g)
    put("pscale", 0, pool_scale[0])
    for tap in range(3):
        put("scw", tap * 16, sconv_w[0, tap])
    put("scb", 0, sconv_b[0])
    put("c", 0, c_row)
    v[:, VOFF["mask"]:VOFF["mask"] + HALO] = 0.0 if q == 0 else 1.0
    for gi, win in enumerate(POOL_WINDOWS):
        o = VOFF["corr"] + gi * 16
        if q == 0:
            cnt = np.minimum(np.arange(16) + 1, win).astype(np.float32)
            v[:, o:o + 16] = (np.float32(win) / cnt)[None, :]
        else:
            v[:, o:o + 16] = 1.0
    return v


_PROG_CACHE = {}


def run(x, c, ada_w, ada_b, norm1_g, norm2_g, pool_w, pool_scale, bcx_w, sconv_w, sconv_b, sout_w, up_w,
        fconv_w, fconv_b, down_w, final_g, trace=False):
    x = np.asarray(x, np.float32)
    B, S, _ = x.shape
    QN = 8 // B
    CW = S // QN
    NT = CW // WV
    assert NT * WV * QN == S
    if NT not in _PROG_CACHE:
        _PROG_CACHE[NT] = build_program(NT)
    nc = _PROG_CACHE[NT]
    f = lambda a: np.ascontiguousarray(np.asarray(a, np.float32))
    shared = {"ada_w": f(ada_w), "pool_w": f(pool_w), "bcx_w": f(bcx_w), "sout_w": f(sout_w),
              "up_w": f(up_w), "down_w": f(down_w)}
    in_maps = []
    for core in range(8):
        b, q = core // QN, core % QN
        xs = np.zeros((D, CW + HALO), np.float32)
        lo = q * CW - HALO
        if lo < 0:
            xs[:, HALO:] = x[b, 0:CW, :].T
        else:
            xs[:, :] = x[b, lo:lo + CW + HALO, :].T
        m = dict(shared)
        m["xT"] = xs
        m["vecs"] = pack_vecs(np.asarray(c, np.float32)[b], q, np.asarray(ada_b), np.asarray(norm1_g), np.asarray(norm2_g),
                              np.asarray(final_g), np.asarray(pool_scale), np.asarray(sconv_w), np.asarray(sconv_b),
                              np.asarray(fconv_w), np.asarray(fconv_b))
        in_maps.append(m)
    res = run_bass_kernel_spmd(nc, in_maps, core_ids=list(range(8)), **({"trace": True} if trace else {}))
    out = np.empty((B, S, D), np.float32)
    for core in range(8):
        b, q = core // QN, core % QN
        out[b, q * CW:(q + 1) * CW, :] = res.results[core]["outT"].T
    return out, res


def kernel(**inputs):
    out, _ = run(**inputs)
    return out
```

```python
import numpy as np
import concourse.bass as bass
import concourse.mybir as mybir
from concourse.bass_utils import run_bass_kernel_spmd

F32 = mybir.dt.float32
BF16 = mybir.dt.bfloat16
AF = mybir.ActivationFunctionType
ALU = mybir.AluOpType

D = 2048
KC = 16
FF = 5632
FC = 44
HALO = 23
WV = 1024
T = WV + HALO
NBLK = 3
BW = T // NBLK
EPS = 1e-6
NS = 4
NTMP = 5
POOL_WINDOWS = (2, 4, 8, 16)
FFB = [8, 8, 8, 8, 8, 4]

VOFF = {}
_o = 0
for _n, _s in [("adab", 192), ("n1g", 32), ("n2g", 32), ("fing", 16), ("pscale", 16), ("scw", 48),
               ("scb", 16), ("fcw", 528), ("fcb", 176), ("c", 16), ("mask", 32), ("corr", 64)]:
    VOFF[_n] = _o
    _o += _s
NV = _o


class _Eng:
    def __init__(self, name):
        self.name = name
        self.ops = []
        self.count = 0
        self.sem = None
        self.waited = {}


class Gen:
    def __init__(self, nc, NT):
        self.nc = nc
        self.NT = NT
        self.eng = {n: _Eng(n) for n in ("pe", "act", "dve", "pool", "sp")}
        self.last_w = {}
        self.readers = {}
        self.dma_cnt = {}
        self.gcount = 0
        self.tmp_ctr = 0
        self.slot_ctr = 0
        self.bg_todo = []
        self.bg_pending = None
        self.bg_cast = None

    def op(self, eng, build, reads=(), writes=(), signal=True, dma_sem=None):
        E = self.eng[eng]
        is_dma = dma_sem is not None
        waits = {}

        def need(tok, raw):
            if tok is None:
                return
            sem, val, kind = tok
            if kind != "dma" and not is_dma and kind == eng and eng == "pe":
                return
            if kind != "dma":
                assert val <= self.eng[kind].count, ("deferred token not yet signalled", eng, kind)
            if E.waited.get(id(sem), 0) >= val:
                return
            if waits.get(id(sem), (None, 0))[1] < val:
                waits[id(sem)] = (sem, val)

        for k in reads:
            need(self.last_w.get(k), True)
        for k in writes:
            need(self.last_w.get(k), False)
            for tok in self.readers.get(k, {}).values():
                need(tok, False)
        wl = list(waits.values())
        for sem, val in wl:
            E.waited[id(sem)] = val
        if is_dma:
            self.dma_cnt[id(dma_sem)] = self.dma_cnt.get(id(dma_sem), 0) + 16
            tok = (dma_sem, self.dma_cnt[id(dma_sem)], "dma")
            inc = (dma_sem, 16)
        elif signal:
            E.count += 1
            tok = (E.sem, E.count, eng)
            inc = (E.sem, 1)
        else:
            tok = (E.sem, E.count + 1, eng)
            inc = None
        for k in reads:
            r = self.readers.setdefault(k, {})
            r[id(tok[0])] = tok
        for k in writes:
            self.last_w[k] = tok
            self.readers[k] = {}
        E.ops.append((wl, build, inc))
        return tok

    def replay(self, eng, e):
        for wl, build, inc in self.eng[eng].ops:
            for sem, val in wl:
                e.wait_ge(sem, val)
            ins = build(e)
            if inc is not None:
                ins.then_inc(inc[0], inc[1])

    def new_tmp(self):
        s = self.tmp_ctr % NTMP
        self.tmp_ctr += 1
        return s

    def tmp(self, s):
        return self.TMP[:, s, :]

    def load_slot(self, src, kk, nn):
        s = self.slot_ctr % NS
        self.slot_ctr += 1
        dst = self.WR[:, s, 0:kk * nn].rearrange("p (k n) -> p k n", n=nn)
        srcv = src.rearrange("(k p) n -> p k n", p=128)
        self.op("pool", lambda e: e.dma_start(out=dst, in_=srcv), writes=[("W", s)], dma_sem=self.SW[s])
        return s, dst

    def pe_group(self, ksteps):
        st = self.gcount % 2
        self.gcount += 1
        n = len(ksteps)
        for i, (lhsT, rhs_fn, rk) in enumerate(ksteps):
            first, last = (i == 0), (i == n - 1)

            def build(e, lhsT=lhsT, rhs_fn=rhs_fn, first=first, last=last, st=st):
                ins = None
                for b in range(NBLK):
                    bank = st * 3 + b
                    ins = e.matmul(self.PS[:, bank * 512:bank * 512 + BW], lhsT, rhs_fn(b),
                                   start=first, stop=last)
                return ins
            self.op("pe", build, reads=rk, writes=[("PS", st)], signal=last)
        view = self.PS[:, st * 1536:(st + 1) * 1536].rearrange("p (b c) -> p b c", c=512)[:, :, 0:BW]
        return st, view

    def evac(self, st, view, scale=None, extra_reads=()):
        s = self.new_tmp()
        out = self.TMP[:, s, :].rearrange("p (b c) -> p b c", c=BW)
        if scale is None:
            self.op("act", lambda e: e.activation(out=out, in_=view, func=AF.Identity),
                    reads=[("PS", st)], writes=[("TMP", s)])
        else:
            self.op("act", lambda e: e.activation(out=out, in_=view, func=AF.Identity, scale=scale),
                    reads=[("PS", st)] + list(extra_reads), writes=[("TMP", s)])
        return s

    def blkfn(self, ap2d):
        return lambda b: ap2d[:, b * BW:(b + 1) * BW]

    def vec(self, name, idx=0, n=1):
        o = VOFF[name] + idx
        return self.VEC[:, o:o + n]

    def ada_load(self, l, j):
        return self.load_slot(self.ada_w[l, :, 256 * j:256 * j + 256], 16, 256)

    def ada_mm(self, l, j, s, dst):
        for mm in range(2):
            m = 2 * j + mm
            part_bank = 6 if (m < 32 or 48 <= m < 80) else 7
            col = m if m < 32 else (m - 32 if m < 48 else (m - 48 + 32 if m < 80 else m - 80 + 16))
            for k in range(16):
                def build(e, k=k, mm=mm, dst=dst, bank=part_bank, col=col):
                    return e.matmul(self.PS[:, bank * 512 + col:bank * 512 + col + 1],
                                    dst[:, k, mm * 128:(mm + 1) * 128], self.CB[:, k:k + 1],
                                    start=(k == 0), stop=(k == 15))
                self.op("pe", build, reads=[("W", s), ("CB",)], writes=[("PS", part_bank)], signal=(k == 15))

    def ada_block(self, l, j):
        s, dst = self.ada_load(l, j)
        self.ada_mm(l, j, s, dst)

    def bg_step(self):
        if self.bg_cast is not None:
            l, m = self.bg_cast
            self.bg_cast = None
            part_bank = 6 if (m < 32 or 48 <= m < 80) else 7
            col = m if m < 32 else (m - 32 if m < 48 else (m - 48 + 32 if m < 80 else m - 80 + 16))
            for k in range(16):
                def build(e, k=k, bank=part_bank, col=col):
                    return e.matmul(self.PS[:, bank * 512 + col:bank * 512 + col + 1],
                                    self.ADAB[:, k, :], self.CB[:, k:k + 1], start=(k == 0), stop=(k == 15))
                self.op("pe", build, reads=[("ADAB",), ("CB",)], writes=[("PS", part_bank)], signal=(k == 15))
        if self.bg_pending is not None:
            self.bg_cast = self.bg_pending
            self.bg_pending = None
            self.op("dve", lambda e: e.tensor_copy(out=self.ADAB[:, :, :], in_=self.ADAF[:, :, :]),
                    reads=[("ADAF",)], writes=[("ADAB",)])
        if self.bg_todo:
            l, m = self.bg_todo.pop(0)
            srcv = self.ada_w[l, :, 128 * m:128 * m + 128].rearrange("(k p) n -> p k n", p=128)
            self.op("sp", lambda e: e.dma_start(out=self.ADAF[:, :, :], in_=srcv), writes=[("ADAF",)], dma_sem=self.SA)
            self.bg_pending = (l, m)

    def ada_part(self, l, part):
        m0, m1, bank, c0 = {"a": (0, 32, 6, 0), "b": (32, 48, 7, 0), "c": (48, 80, 6, 32), "d": (80, 96, 7, 16)}[part]
        n = m1 - m0
        raw = self.MODRAW[:, m0:m1]
        mod = self.MOD[:, l, m0:m1]
        self.op("act", lambda e: e.activation(out=raw, in_=self.PS[:, bank * 512 + c0:bank * 512 + c0 + n],
                                              func=AF.Identity),
                reads=[("PS", bank)], writes=[("MODRAW", part)])
        bia = self.vec("adab", l * 96 + m0, n)
        self.op("dve", lambda e: e.tensor_tensor(out=mod, in0=raw, in1=bia, op=ALU.add),
                reads=[("MODRAW", part), ("VEC",)], writes=[("MOD", l, part)])
        DER = self.DER
        if part == "a":
            gn = self.vec("n1g", l * 16, 16)
            self.op("dve", lambda e: e.scalar_tensor_tensor(out=DER[:, l, 0, :], in0=self.MOD[:, l, 16:32], scalar=1.0,
                                                            in1=gn, op0=ALU.add, op1=ALU.mult),
                    reads=[("MOD", l, part), ("VEC",)], writes=[("DER", l, 0)])
        elif part == "b":
            if l == 0:
                ps = self.vec("pscale", 0, 16)
                self.op("dve", lambda e: e.scalar_tensor_tensor(out=DER[:, l, 2, :], in0=self.MOD[:, l, 32:48], scalar=1.0,
                                                                in1=ps, op0=ALU.add, op1=ALU.mult),
                        reads=[("MOD", l, part), ("VEC",)], writes=[("DER", l, 2)])
            else:
                self.op("dve", lambda e: e.tensor_scalar(out=DER[:, l, 2, :], in0=self.MOD[:, l, 32:48], scalar1=1.0,
                                                         scalar2=None, op0=ALU.add),
                        reads=[("MOD", l, part)], writes=[("DER", l, 2)])
        elif part == "c":
            gn = self.vec("n2g", l * 16, 16)
            self.op("dve", lambda e: e.scalar_tensor_tensor(out=DER[:, l, 3, :], in0=self.MOD[:, l, 64:80], scalar=1.0,
                                                            in1=gn, op0=ALU.add, op1=ALU.mult),
                    reads=[("MOD", l, part), ("VEC",)], writes=[("DER", l, 3)])
        else:
            self.op("dve", lambda e: e.tensor_scalar(out=DER[:, l, 5, :], in0=self.MOD[:, l, 80:96], scalar1=1.0,
                                                     scalar2=None, op0=ALU.add),
                    reads=[("MOD", l, part)], writes=[("DER", l, 5)])

    def coef(self, l, which):
        if which == "A1":
            return self.DER[:, l, 0, :], [("DER", l, 0)]
        if which == "B1":
            return self.MOD[:, l, 0:16], [("MOD", l, "a")]
        if which == "G1":
            return self.DER[:, l, 2, :], [("DER", l, 2)]
        if which == "A2":
            return self.DER[:, l, 3, :], [("DER", l, 3)]
        if which == "B2":
            return self.MOD[:, l, 48:64], [("MOD", l, "c")]
        if which == "G2":
            return self.DER[:, l, 5, :], [("DER", l, 5)]
        raise KeyError(which)

    def norm_stats(self, act_only=False):
        X, H = self.X, self.H
        for k in range(KC):
            if k % 2 == 0 or act_only:
                self.op("act", lambda e, k=k: e.activation(out=H[:, k, :], in_=X[:, k, :], func=AF.Square),
                        reads=[("X", k)], writes=[("H", k)])
            else:
                self.op("dve", lambda e, k=k: e.tensor_tensor(out=H[:, k, :], in0=X[:, k, :], in1=X[:, k, :], op=ALU.mult),
                        reads=[("X", k)], writes=[("H", k)])
        steps = [(self.ONES[:, :], self.blkfn(H[:, k, :]), [("H", k), ("ONES",)]) for k in range(KC)]
        st, view = self.pe_group(steps)
        s = self.new_tmp()
        ts = self.tmp(s)
        out3 = self.TMP[:, s, :].rearrange("p (b c) -> p b c", c=BW)
        self.op("act", lambda e: e.activation(out=out3, in_=view, func=AF.Sqrt, bias=self.EPSV[:, 0:1]),
                reads=[("PS", st), ("EPSV",)], writes=[("TMP", s)])
        self.op("dve", lambda e: e.reciprocal(out=self.RSTD[:, :], in_=ts),
                reads=[("TMP", s)], writes=[("RSTD",)])

    def modulate(self, l, site, first_tile, to_h=True):
        A, ak = self.coef(l, "A%d" % site)
        Bv, bk = self.coef(l, "B%d" % site)
        X, H = self.X, self.H
        for k in range(KC):
            s = self.new_tmp()
            ts = self.tmp(s)
            self.op("dve", lambda e, k=k, ts=ts: e.tensor_tensor(out=ts, in0=X[:, k, :], in1=self.RSTD[:, :], op=ALU.mult),
                    reads=[("X", k), ("RSTD",)], writes=[("TMP", s)])
            self.op("act", lambda e, k=k, ts=ts: e.activation(out=H[:, k, :], in_=ts, func=AF.Identity,
                                                              bias=Bv[:, k:k + 1], scale=A[:, k:k + 1]),
                    reads=[("TMP", s)] + ak + bk, writes=[("H", k)])
            if first_tile:
                self.op("dve", lambda e, k=k: e.tensor_tensor(out=H[:, k, 0:HALO], in0=H[:, k, 0:HALO],
                                                              in1=self.MASKB[:, :], op=ALU.mult),
                        reads=[("H", k), ("MASKB",)], writes=[("H", k)])

    def resid_evac(self, st, view, l, which, m):
        G, gk = self.coef(l, which)
        s = self.evac(st, view, scale=G[:, m:m + 1], extra_reads=gk)
        ts = self.tmp(s)
        X = self.X
        self.op("dve", lambda e: e.tensor_tensor(out=X[:, m, :], in0=X[:, m, :], in1=ts, op=ALU.add),
                reads=[("TMP", s), ("X", m)], writes=[("X", m)])

    def conv3(self, src_s, w_ap3, b_ap):
        src = self.tmp(src_s)
        d = self.new_tmp()
        dst = self.tmp(d)
        self.op("dve", lambda e: e.tensor_scalar(out=dst, in0=src, scalar1=w_ap3[2], scalar2=b_ap,
                                                 op0=ALU.mult, op1=ALU.add),
                reads=[("TMP", src_s), ("VEC",)], writes=[("TMP", d)])
        self.op("dve", lambda e: e.scalar_tensor_tensor(out=dst[:, 1:T], in0=src[:, 0:T - 1], scalar=w_ap3[1],
                                                        in1=dst[:, 1:T], op0=ALU.mult, op1=ALU.add),
                reads=[("TMP", src_s), ("TMP", d), ("VEC",)], writes=[("TMP", d)])
        self.op("dve", lambda e: e.scalar_tensor_tensor(out=dst[:, 2:T], in0=src[:, 0:T - 2], scalar=w_ap3[0],
                                                        in1=dst[:, 2:T], op0=ALU.mult, op1=ALU.add),
                reads=[("TMP", src_s), ("TMP", d), ("VEC",)], writes=[("TMP", d)])
        return d

    def tref(self, ref):
        kind, u = ref
        if kind == "TMP":
            return self.TMP[:, u, :], [("TMP", u)]
        return self.PT[:, u, :], [("MID", (2 * u) // 8, (2 * u) % 8), ("MID", (2 * u + 1) // 8, (2 * u + 1) % 8)]

    def pool_front(self, first_tile):
        l = 0
        X, H = self.X, self.H
        A, ak = self.coef(l, "A1")
        Bv, bk = self.coef(l, "B1")
        self.norm_stats(act_only=not first_tile)
        dsets = [[("PT", 0), ("PT", 1), ("PT", 2)], [("PT", 3), ("PT", 4), ("PT", 5)]]
        psets = [[("PT", 6), ("PT", 7), ("TMP", 0)], [("TMP", 1), ("TMP", 2), ("TMP", 3)]]

        def stage1(k, ve, st):
            t, tk = self.tref(st[0])
            self.op(ve, lambda e: e.tensor_tensor(out=t, in0=X[:, k, :], in1=self.RSTD[:, :], op=ALU.mult),
                    reads=[("X", k), ("RSTD",)], writes=tk)
            self.op("act", lambda e: e.activation(out=t, in_=t, func=AF.Identity, bias=Bv[:, k:k + 1], scale=A[:, k:k + 1]),
                    reads=tk + ak + bk, writes=tk)
            if first_tile:
                self.op(ve, lambda e: e.tensor_tensor(out=t[:, 0:HALO], in0=t[:, 0:HALO], in1=self.vec("mask", 0, HALO),
                                                      op=ALU.mult), reads=tk + [("VEC",)], writes=tk)

        def stage2(k, ve, st):
            g = k // 4
            win = POOL_WINDOWS[g]
            cur, ck = self.tref(st[0])
            dd, i = 1, 0
            while dd < win:
                nx, nk = self.tref(st[1 + (i % 2)])
                self.op(ve, lambda e, cur=cur, nx=nx, dd=dd: e.tensor_tensor(out=nx[:, dd:T], in0=cur[:, dd:T],
                                                                              in1=cur[:, 0:T - dd], op=ALU.add),
                        reads=ck, writes=nk)
                cur, ck = nx, nk
                dd *= 2
                i += 1
            if first_tile:
                self.op(ve, lambda e, cur=cur: e.tensor_tensor(out=cur[:, HALO:HALO + 16], in0=cur[:, HALO:HALO + 16],
                                                               in1=self.vec("corr", g * 16, 16), op=ALU.mult),
                        reads=ck + [("VEC",)], writes=ck)
            return cur, ck

        def final(k, st, cur, ck):
            win = POOL_WINDOWS[k // 4]
            t, tk = self.tref(st[0])
            self.op("dve", lambda e: e.scalar_tensor_tensor(out=H[:, k, :], in0=cur, scalar=1.0 / win, in1=t,
                                                            op0=ALU.mult, op1=ALU.subtract),
                    reads=ck + tk, writes=[("H", k)])

        for g in range(4):
            c = [4 * g, 4 * g + 1, 4 * g + 2, 4 * g + 3]
            if g == 0:
                stage1(c[0], "dve", dsets[0])
                stage1(c[1], "dve", dsets[1])
            stage1(c[2], "dve", psets[0])
            stage1(c[3], "dve", psets[1])
            r0 = stage2(c[0], "dve", dsets[0])
            final(c[0], dsets[0], *r0)
            r1 = stage2(c[1], "dve", dsets[1])
            final(c[1], dsets[1], *r1)
            if g < 3:
                stage1(c[0] + 4, "dve", dsets[0])
                stage1(c[1] + 4, "dve", dsets[1])
            r2 = stage2(c[2], "dve", psets[0])
            r3 = stage2(c[3], "dve", psets[1])
            final(c[2], psets[0], *r2)
            final(c[3], psets[1], *r3)

    def pool_w_load(self):
        out = []
        for gp in range(2):
            srcs = []
            s = self.slot_ctr % NS
            self.slot_ctr += 1
            for gi in range(2):
                g = 2 * gp + gi
                dst = self.WR[:, s, gi * 2048:(gi + 1) * 2048].rearrange("p (k n) -> p k n", n=512)
                srcv = self.pool_w[0, g].rearrange("(k p) n -> p k n", p=128)
                self.op("pool", lambda e, dst=dst, srcv=srcv: e.dma_start(out=dst, in_=srcv),
                        writes=[("W", s)], dma_sem=self.SW[s])
                srcs.append(dst)
            out.append((s, srcs))
        return out

    def pool_back(self, pre=None):
        H = self.H
        if pre is None:
            pre = self.pool_w_load()
        for gp in range(2):
            s, srcs = pre[gp]
            for gi in range(2):
                g = 2 * gp + gi
                wv = srcs[gi]
                for mc in range(4):
                    steps = [(wv[:, kc, mc * 128:(mc + 1) * 128], self.blkfn(H[:, 4 * g + kc, :]),
                              [("H", 4 * g + kc), ("W", s)]) for kc in range(4)]
                    st, view = self.pe_group(steps)
                    self.resid_evac(st, view, 0, "G1", 4 * g + mc)

    def ffn(self, l, first_tile):
        H, MID = self.H, self.MID
        self.norm_stats()
        self.modulate(l, 2, first_tile)
        upw, dnw = self.up_w, self.down_w

        def up_block(b):
            nb = FFB[b]
            j0 = sum(FFB[:b])
            for jj in range(0, nb, 2):
                j = j0 + jj
                sg, wg = self.load_slot(upw[l, :, 128 * j:128 * j + 256], 16, 256)
                sv, wv = self.load_slot(upw[l, :, FF + 128 * j:FF + 128 * j + 256], 16, 256)
                for e2 in range(2):
                    jc = j + e2
                    steps = [(wg[:, k, e2 * 128:(e2 + 1) * 128], self.blkfn(H[:, k, :]), [("H", k), ("W", sg)])
                             for k in range(KC)]
                    st, view = self.pe_group(steps)
                    u = self.evac(st, view)
                    w3 = [self.vec("fcw", l * 264 + tap * 88 + jc, 1) for tap in range(3)]
                    ag = self.conv3(u, w3, self.vec("fcb", l * 88 + jc, 1))
                    tg = self.tmp(ag)
                    self.op("act", lambda e, tg=tg: e.activation(out=tg, in_=tg, func=AF.Silu),
                            reads=[("TMP", ag)], writes=[("TMP", ag)])
                    steps = [(wv[:, k, e2 * 128:(e2 + 1) * 128], self.blkfn(H[:, k, :]), [("H", k), ("W", sv)])
                             for k in range(KC)]
                    st, view = self.pe_group(steps)
                    u = self.evac(st, view)
                    w3 = [self.vec("fcw", l * 264 + tap * 88 + FC + jc, 1) for tap in range(3)]
                    av = self.conv3(u, w3, self.vec("fcb", l * 88 + FC + jc, 1))
                    tv = self.tmp(av)
                    mo = MID[:, b % 2, jj + e2, :]
                    self.op("dve", lambda e, tg=tg, tv=tv, mo=mo: e.tensor_tensor(out=mo, in0=tg, in1=tv, op=ALU.mult),
                            reads=[("TMP", ag), ("TMP", av)], writes=[("MID", b % 2, jj + e2)])
                    self.bg_step()

        def down_block(b):
            nb = FFB[b]
            j0 = sum(FFB[:b])
            for q in range(4):
                s, wd = self.load_slot(dnw[l, 128 * j0:128 * (j0 + nb), 512 * q:512 * q + 512], nb, 512)
                for mc in range(4):
                    m = 4 * q + mc
                    steps = [(wd[:, kc, mc * 128:(mc + 1) * 128], self.blkfn(MID[:, b % 2, kc, :]),
                              [("MID", b % 2, kc), ("W", s)]) for kc in range(nb)]
                    st, view = self.pe_group(steps)
                    self.resid_evac(st, view, l, "G2", m)
                    if mc % 2 == 1:
                        self.bg_step()

        nbk = len(FFB)
        up_block(0)
        for b in range(1, nbk):
            up_block(b)
            down_block(b - 1)
        down_block(nbk - 1)

    def sconv_layer(self, first_tile):
        l = 1
        H, BV = self.H, self.BV
        self.norm_stats()
        self.modulate(l, 1, first_tile)
        bw = self.bcx_w
        for j in range(0, KC, 2):
            sc, wc = self.load_slot(bw[0, :, D + 128 * j:D + 128 * j + 256], 16, 256)
            su, wu = self.load_slot(bw[0, :, 2 * D + 128 * j:2 * D + 128 * j + 256], 16, 256)
            sb, wb = self.load_slot(bw[0, :, 128 * j:128 * j + 256], 16, 256)
            for e2 in range(2):
                jc = j + e2
                steps = [(wc[:, k, e2 * 128:(e2 + 1) * 128], self.blkfn(H[:, k, :]), [("H", k), ("W", sc)]) for k in range(KC)]
                st, view = self.pe_group(steps)
                u1 = self.evac(st, view)
                steps = [(wu[:, k, e2 * 128:(e2 + 1) * 128], self.blkfn(H[:, k, :]), [("H", k), ("W", su)]) for k in range(KC)]
                st, view = self.pe_group(steps)
                u2 = self.evac(st, view)
                t1, t2 = self.tmp(u1), self.tmp(u2)
                self.op("dve", lambda e, t1=t1, t2=t2: e.tensor_tensor(out=t1, in0=t1, in1=t2, op=ALU.mult),
                        reads=[("TMP", u1), ("TMP", u2)], writes=[("TMP", u1)])
                w3 = [self.vec("scw", tap * 16 + jc, 1) for tap in range(3)]
                v = self.conv3(u1, w3, self.vec("scb", jc, 1))
                tv = self.tmp(v)
                steps = [(wb[:, k, e2 * 128:(e2 + 1) * 128], self.blkfn(H[:, k, :]), [("H", k), ("W", sb)]) for k in range(KC)]
                st, view = self.pe_group(steps)
                u3 = self.evac(st, view)
                t3 = self.tmp(u3)
                bo = BV[:, jc, :]
                self.op("dve", lambda e, t3=t3, tv=tv, bo=bo: e.tensor_tensor(out=bo, in0=t3, in1=tv, op=ALU.mult),
                        reads=[("TMP", u3), ("TMP", v)], writes=[("MID", jc // 8, jc % 8)])
        for mp in range(0, KC, 2):
            s, ws = self.load_slot(self.sout_w[0, :, 128 * mp:128 * mp + 256], 16, 256)
            for e2 in range(2):
                m = mp + e2
                steps = [(ws[:, k, e2 * 128:(e2 + 1) * 128], self.blkfn(BV[:, k, :]),
                          [("MID", k // 8, k % 8), ("W", s)]) for k in range(KC)]
                st, view = self.pe_group(steps)
                self.resid_evac(st, view, 1, "G1", m)

    def final_norm(self, i):
        X = self.X
        self.norm_stats()
        for k in range(KC):
            s = self.new_tmp()
            ts = self.tmp(s)
            fg = self.vec("fing", k, 1)
            self.op("dve", lambda e, k=k, ts=ts, fg=fg: e.scalar_tensor_tensor(
                out=ts, in0=X[:, k, :], scalar=fg, in1=self.RSTD[:, :], op0=ALU.mult, op1=ALU.mult),
                reads=[("X", k), ("RSTD",), ("VEC",)], writes=[("TMP", s)])
            dst = self.outT[k * 128:(k + 1) * 128, i * WV:(i + 1) * WV]
            self.op("sp", lambda e, ts=ts, dst=dst: e.dma_start(out=dst, in_=ts[:, HALO:T]),
                    reads=[("TMP", s)], dma_sem=self.SO[s])

    def generate(self):
        self.op("sp", lambda e: e.dma_start(out=self.VEC[:, :], in_=self.vecs), writes=[("VEC",)], dma_sem=self.SV)
        self.op("dve", lambda e: e.memset(self.ONES[:, :], 1.0 / D), writes=[("ONES",)])
        self.op("dve", lambda e: e.memset(self.EPSV[:, :], EPS), writes=[("EPSV",)])
        self.op("dve", lambda e: e.memset(self.TMP[:, :, :], 0.0), writes=[("TMP", t) for t in range(NTMP)])
        self.op("dve", lambda e: e.memset(self.PT[:, :, :], 0.0), writes=[("MID", a, b) for a in range(2) for b in range(8)])
        self.op("dve", lambda e: e.tensor_copy(out=self.CB[:, :], in_=self.vec("c", 0, 16)), reads=[("VEC",)], writes=[("CB",)])
        self.op("dve", lambda e: e.tensor_copy(out=self.MASKB[:, :], in_=self.vec("mask", 0, HALO)),
                reads=[("VEC",)], writes=[("MASKB",)])
        for i in range(self.NT):
            first = (i == 0)
            for k in range(KC):
                self.op("sp", lambda e, i=i, k=k: e.dma_start(
                    out=self.X[:, k, :], in_=self.xT[k * 128:(k + 1) * 128, i * WV:i * WV + T]),
                    writes=[("X", k)], dma_sem=self.SXK[k])
            if first:
                for j in range(16):
                    self.ada_block(0, j)
                self.ada_part(0, "a")
            pre = None if first else self.pool_w_load()
            self.pool_front(first)
            if first:
                for j in range(16, 24):
                    self.ada_block(0, j)
                self.ada_part(0, "b")
                for j in range(24, 40):
                    self.ada_block(0, j)
                self.ada_part(0, "c")
                for j in range(40, 48):
                    self.ada_block(0, j)
                self.ada_part(0, "d")
                self.bg_todo = [(1, m) for m in range(96)]
            self.pool_back(pre)
            self.ffn(0, first)
            if first:
                while self.bg_todo or self.bg_pending is not None or self.bg_cast is not None:
                    self.bg_step()
                for part in "abcd":
                    self.ada_part(1, part)
            self.sconv_layer(first)
            self.ffn(1, first)
            self.final_norm(i)
        finals = [(self.SO[s], self.dma_cnt.get(id(self.SO[s]), 0)) for s in range(NTMP)]

        def fin(e):
            ins = None
            for sem, val in finals:
                if val > 0:
                    ins = e.wait_ge(sem, val)
            return ins
        self.eng["sp"].ops.append(([], fin, None))


def build_program(NT):
    CW = NT * WV
    nc = bass.Bass("TRN2", target_bir_lowering=False)
    g = Gen(nc, NT)
    g.xT = nc.dram_tensor("xT", [D, CW + HALO], F32, kind="ExternalInput").ap()
    g.vecs = nc.dram_tensor("vecs", [128, NV], F32, kind="ExternalInput").ap()
    g.ada_w = nc.dram_tensor("ada_w", [2, D, 6 * D], F32, kind="ExternalInput").ap()
    g.pool_w = nc.dram_tensor("pool_w", [1, 4, 512, 512], F32, kind="ExternalInput").ap()
    g.bcx_w = nc.dram_tensor("bcx_w", [1, D, 3 * D], F32, kind="ExternalInput").ap()
    g.sout_w = nc.dram_tensor("sout_w", [1, D, D], F32, kind="ExternalInput").ap()
    g.up_w = nc.dram_tensor("up_w", [2, D, 2 * FF], F32, kind="ExternalInput").ap()
    g.down_w = nc.dram_tensor("down_w", [2, FF, D], F32, kind="ExternalInput").ap()
    g.outT = nc.dram_tensor("outT", [D, CW], F32, kind="ExternalOutput").ap()
    from contextlib import ExitStack
    with ExitStack() as es:
        def sb(name, shape, dt):
            return es.enter_context(nc.sbuf_tensor(name, shape, dt))
        g.X = sb("X", [128, KC, T], F32)
        g.H = sb("H", [128, KC, T], BF16)
        g.MIDF = sb("MID", [128, 16 * T], BF16)
        g.MID = g.MIDF[:, :].rearrange("p (a b t) -> p a b t", a=2, b=8)
        g.BV = g.MIDF[:, :].rearrange("p (c t) -> p c t", t=T)
        g.PT = g.MIDF[:, :].bitcast(F32).rearrange("p (u t) -> p u t", t=T)
        g.WR = sb("WR", [128, NS, 4096], BF16)
        g.TMP = sb("TMP", [128, NTMP, T], F32)
        g.RSTD = sb("RSTD", [128, T], F32)
        g.VEC = sb("VEC", [128, NV], F32)
        g.MODRAW = sb("MODRAW", [128, 96], F32)
        g.MOD = sb("MOD", [128, 2, 96], F32)
        g.DER = sb("DER", [128, 2, 6, 16], F32)
        g.ONES = sb("ONES", [128, 128], BF16)
        g.CB = sb("CB", [128, 16], BF16)
        g.MASKB = sb("MASKB", [128, HALO], BF16)
        g.EPSV = sb("EPSV", [128, 1], F32)
        g.ADAF = sb("ADAF", [128, 16, 128], F32)
        g.ADAB = sb("ADAB", [128, 16, 128], BF16)
        g.PS = es.enter_context(nc.psum_tensor("PS", [128, 4096], F32))
        for n in g.eng:
            g.eng[n].sem = es.enter_context(nc.semaphore("prog_" + n))
        g.SW = [es.enter_context(nc.semaphore("sw%d" % s)) for s in range(NS)]
        g.SO = [es.enter_context(nc.semaphore("so%d" % s)) for s in range(NTMP)]
        g.SXK = [es.enter_context(nc.semaphore("sx%d" % k)) for k in range(KC)]
        g.SV = es.enter_context(nc.semaphore("sv"))
        g.SA = es.enter_context(nc.semaphore("sa"))
        g.generate()
        with nc.Block() as block:
            @block.tensor
            def _(e):
                g.replay("pe", e)

            @block.scalar
            def _(e):
                g.replay("act", e)

            @block.vector
            def _(e):
                g.replay("dve", e)

            @block.gpsimd
            def _(e):
                g.replay("pool", e)

            @block.sync
            def _(e):
                g.replay("sp", e)
    return nc


def pack_vecs(c_row, q, ada_b, norm1_g, norm2_g, final_g, pool_scale, sconv_w, sconv_b, fconv_w, fconv_b):
    v = np.zeros((128, NV), np.float32)

    def put(name, off, arr):
        o = VOFF[name] + off
        a = np.asarray(arr, np.float32).reshape(-1, 128).T
        v[:, o:o + a.shape[1]] = a
    for l in range(2):
        put("adab", l * 96, ada_b[l])
        put("n1g", l * 16, norm1_g[l])
        put("n2g", l * 16, norm2_g[l])
        for tap in range(3):
            put("fcw", l * 264 + tap * 88, fconv_w[l, tap])
        put("fcb", l * 88, fconv_b[l])
    put("fing", 0, final_g)
    put("pscale", 0, pool_scale[0])
    for tap in range(3):
        put("scw", tap * 16, sconv_w[0, tap])
    put("scb", 0, sconv_b[0])
    put("c", 0, c_row)
    v[:, VOFF["mask"]:VOFF["mask"] + HALO] = 0.0 if q == 0 else 1.0
    for gi, win in enumerate(POOL_WINDOWS):
        o = VOFF["corr"] + gi * 16
        if q == 0:
            cnt = np.minimum(np.arange(16) + 1, win).astype(np.float32)
            v[:, o:o + 16] = (np.float32(win) / cnt)[None, :]
        else:
            v[:, o:o + 16] = 1.0
    return v


_PROG_CACHE = {}


def run(x, c, ada_w, ada_b, norm1_g, norm2_g, pool_w, pool_scale, bcx_w, sconv_w, sconv_b, sout_w, up_w,
        fconv_w, fconv_b, down_w, final_g, trace=False):
    x = np.asarray(x, np.float32)
    B, S, _ = x.shape
    QN = 8 // B
    CW = S // QN
    NT = CW // WV
    assert NT * WV * QN == S
    if NT not in _PROG_CACHE:
        _PROG_CACHE[NT] = build_program(NT)
    nc = _PROG_CACHE[NT]
    f = lambda a: np.ascontiguousarray(np.asarray(a, np.float32))
    shared = {"ada_w": f(ada_w), "pool_w": f(pool_w), "bcx_w": f(bcx_w), "sout_w": f(sout_w),
              "up_w": f(up_w), "down_w": f(down_w)}
    in_maps = []
    for core in range(8):
        b, q = core // QN, core % QN
        xs = np.zeros((D, CW + HALO), np.float32)
        lo = q * CW - HALO
        if lo < 0:
            xs[:, HALO:] = x[b, 0:CW, :].T
        else:
            xs[:, :] = x[b, lo:lo + CW + HALO, :].T
        m = dict(shared)
        m["xT"] = xs
        m["vecs"] = pack_vecs(np.asarray(c, np.float32)[b], q, np.asarray(ada_b), np.asarray(norm1_g), np.asarray(norm2_g),
                              np.asarray(final_g), np.asarray(pool_scale), np.asarray(sconv_w), np.asarray(sconv_b),
                              np.asarray(fconv_w), np.asarray(fconv_b))
        in_maps.append(m)
    res = run_bass_kernel_spmd(nc, in_maps, core_ids=list(range(8)), **({"trace": True} if trace else {}))
    out = np.empty((B, S, D), np.float32)
    for core in range(8):
        b, q = core // QN, core % QN
        out[b, q * CW:(q + 1) * CW, :] = res.results[core]["outT"].T
    return out, res


def kernel(**inputs):
    out, _ = run(**inputs)
    return out
```

```python
import numpy as np
import concourse.bass as bass
import concourse.mybir as mybir
from concourse.bass_utils import run_bass_kernel_spmd

F32 = mybir.dt.float32
BF16 = mybir.dt.bfloat16
AF = mybir.ActivationFunctionType
ALU = mybir.AluOpType

D = 2048
KC = 16
FF = 5632
FC = 44
HALO = 23
WV = 1024
T = WV + HALO
NBLK = 3
BW = T // NBLK
EPS = 1e-6
NS = 4
NTMP = 5
POOL_WINDOWS = (2, 4, 8, 16)
FFB = [8, 8, 8, 8, 8, 4]

VOFF = {}
_o = 0
for _n, _s in [("adab", 192), ("n1g", 32), ("n2g", 32), ("fing", 16), ("pscale", 16), ("scw", 48),
               ("scb", 16), ("fcw", 528), ("fcb", 176), ("c", 16), ("mask", 32), ("corr", 64)]:
    VOFF[_n] = _o
    _o += _s
NV = _o


class _Eng:
    def __init__(self, name):
        self.name = name
        self.ops = []
        self.count = 0
        self.sem = None
        self.waited = {}


class Gen:
    def __init__(self, nc, NT):
        self.nc = nc
        self.NT = NT
        self.eng = {n: _Eng(n) for n in ("pe", "act", "dve", "pool", "sp")}
        self.last_w = {}
        self.readers = {}
        self.dma_cnt = {}
        self.gcount = 0
        self.tmp_ctr = 0
        self.slot_ctr = 0
        self.bg_todo = []
        self.bg_pending = None
        self.bg_cast = None

    def op(self, eng, build, reads=(), writes=(), signal=True, dma_sem=None):
        E = self.eng[eng]
        is_dma = dma_sem is not None
        waits = {}

        def need(tok, raw):
            if tok is None:
                return
            sem, val, kind = tok
            if kind != "dma" and not is_dma and kind == eng and eng == "pe":
                return
            if kind != "dma":
                assert val <= self.eng[kind].count, ("deferred token not yet signalled", eng, kind)
            if E.waited.get(id(sem), 0) >= val:
                return
            if waits.get(id(sem), (None, 0))[1] < val:
                waits[id(sem)] = (sem, val)

        for k in reads:
            need(self.last_w.get(k), True)
        for k in writes:
            need(self.last_w.get(k), False)
            for tok in self.readers.get(k, {}).values():
                need(tok, False)
        wl = list(waits.values())
        for sem, val in wl:
            E.waited[id(sem)] = val
        if is_dma:
            self.dma_cnt[id(dma_sem)] = self.dma_cnt.get(id(dma_sem), 0) + 16
            tok = (dma_sem, self.dma_cnt[id(dma_sem)], "dma")
            inc = (dma_sem, 16)
        elif signal:
            E.count += 1
            tok = (E.sem, E.count, eng)
            inc = (E.sem, 1)
        else:
            tok = (E.sem, E.count + 1, eng)
            inc = None
        for k in reads:
            r = self.readers.setdefault(k, {})
            r[id(tok[0])] = tok
        for k in writes:
            self.last_w[k] = tok
            self.readers[k] = {}
        E.ops.append((wl, build, inc))
        return tok

    def replay(self, eng, e):
        for wl, build, inc in self.eng[eng].ops:
            for sem, val in wl:
                e.wait_ge(sem, val)
            ins = build(e)
            if inc is not None:
                ins.then_inc(inc[0], inc[1])

    def new_tmp(self):
        s = self.tmp_ctr % NTMP
        self.tmp_ctr += 1
        return s

    def tmp(self, s):
        return self.TMP[:, s, :]

    def load_slot(self, src, kk, nn):
        s = self.slot_ctr % NS
        self.slot_ctr += 1
        dst = self.WR[:, s, 0:kk * nn].rearrange("p (k n) -> p k n", n=nn)
        srcv = src.rearrange("(k p) n -> p k n", p=128)
        self.op("pool", lambda e: e.dma_start(out=dst, in_=srcv), writes=[("W", s)], dma_sem=self.SW[s])
        return s, dst

    def pe_group(self, ksteps):
        st = self.gcount % 2
        self.gcount += 1
        n = len(ksteps)
        for i, (lhsT, rhs_fn, rk) in enumerate(ksteps):
            first, last = (i == 0), (i == n - 1)

            def build(e, lhsT=lhsT, rhs_fn=rhs_fn, first=first, last=last, st=st):
                ins = None
                for b in range(NBLK):
                    bank = st * 3 + b
                    ins = e.matmul(self.PS[:, bank * 512:bank * 512 + BW], lhsT, rhs_fn(b),
                                   start=first, stop=last)
                return ins
            self.op("pe", build, reads=rk, writes=[("PS", st)], signal=last)
        view = self.PS[:, st * 1536:(st + 1) * 1536].rearrange("p (b c) -> p b c", c=512)[:, :, 0:BW]
        return st, view

    def pe_group_pair(self, stepsA, stepsB):
        assert len(stepsA) == len(stepsB)
        stA = self.gcount % 2
        stB = (self.gcount + 1) % 2
        self.gcount += 2
        n = len(stepsA)
        for i in range(n):
            for steps, st in ((stepsA, stA), (stepsB, stB)):
                lhsT, rhs_fn, rk = steps[i]
                first, last = (i == 0), (i == n - 1)

                def build(e, lhsT=lhsT, rhs_fn=rhs_fn, first=first, last=last, st=st):
                    ins = None
                    for b in range(NBLK):
                        bank = st * 3 + b
                        ins = e.matmul(self.PS[:, bank * 512:bank * 512 + BW], lhsT, rhs_fn(b),
                                       start=first, stop=last)
                    return ins
                self.op("pe", build, reads=rk, writes=[("PS", st)], signal=last)
        out = []
        for st in (stA, stB):
            out.append((st, self.PS[:, st * 1536:(st + 1) * 1536].rearrange("p (b c) -> p b c", c=512)[:, :, 0:BW]))
        return out

    def evac(self, st, view, scale=None, extra_reads=()):
        s = self.new_tmp()
        out = self.TMP[:, s, :].rearrange("p (b c) -> p b c", c=BW)
        if scale is None:
            self.op("act", lambda e: e.activation(out=out, in_=view, func=AF.Identity),
                    reads=[("PS", st)], writes=[("TMP", s)])
        else:
            self.op("act", lambda e: e.activation(out=out, in_=view, func=AF.Identity, scale=scale),
                    reads=[("PS", st)] + list(extra_reads), writes=[("TMP", s)])
        return s

    def blkfn(self, ap2d):
        return lambda b: ap2d[:, b * BW:(b + 1) * BW]

    def vec(self, name, idx=0, n=1):
        o = VOFF[name] + idx
        return self.VEC[:, o:o + n]

    def ada_load(self, l, j):
        return self.load_slot(self.ada_w[l, :, 256 * j:256 * j + 256], 16, 256)

    def ada_mm(self, l, j, s, dst):
        for mm in range(2):
            m = 2 * j + mm
            part_bank = 6 if (m < 32 or 48 <= m < 80) else 7
            col = m if m < 32 else (m - 32 if m < 48 else (m - 48 + 32 if m < 80 else m - 80 + 16))
            for k in range(16):
                def build(e, k=k, mm=mm, dst=dst, bank=part_bank, col=col):
                    return e.matmul(self.PS[:, bank * 512 + col:bank * 512 + col + 1],
                                    dst[:, k, mm * 128:(mm + 1) * 128], self.CB[:, k:k + 1],
                                    start=(k == 0), stop=(k == 15))
                self.op("pe", build, reads=[("W", s), ("CB",)], writes=[("PS", part_bank)], signal=(k == 15))

    def ada_block(self, l, j):
        s, dst = self.ada_load(l, j)
        self.ada_mm(l, j, s, dst)

    def bg_step(self):
        if self.bg_cast is not None:
            l, m = self.bg_cast
            self.bg_cast = None
            part_bank = 6 if (m < 32 or 48 <= m < 80) else 7
            col = m if m < 32 else (m - 32 if m < 48 else (m - 48 + 32 if m < 80 else m - 80 + 16))
            for k in range(16):
                def build(e, k=k, bank=part_bank, col=col):
                    return e.matmul(self.PS[:, bank * 512 + col:bank * 512 + col + 1],
                                    self.ADAB[:, k, :], self.CB[:, k:k + 1], start=(k == 0), stop=(k == 15))
                self.op("pe", build, reads=[("ADAB",), ("CB",)], writes=[("PS", part_bank)], signal=(k == 15))
        if self.bg_pending is not None:
            self.bg_cast = self.bg_pending
            self.bg_pending = None
            self.op("dve", lambda e: e.tensor_copy(out=self.ADAB[:, :, :], in_=self.ADAF[:, :, :]),
                    reads=[("ADAF",)], writes=[("ADAB",)])
        if self.bg_todo:
            l, m = self.bg_todo.pop(0)
            srcv = self.ada_w[l, :, 128 * m:128 * m + 128].rearrange("(k p) n -> p k n", p=128)
            self.op("sp", lambda e: e.dma_start(out=self.ADAF[:, :, :], in_=srcv), writes=[("ADAF",)], dma_sem=self.SA)
            self.bg_pending = (l, m)

    def ada_part(self, l, part):
        m0, m1, bank, c0 = {"a": (0, 32, 6, 0), "b": (32, 48, 7, 0), "c": (48, 80, 6, 32), "d": (80, 96, 7, 16)}[part]
        n = m1 - m0
        raw = self.MODRAW[:, m0:m1]
        mod = self.MOD[:, l, m0:m1]
        self.op("act", lambda e: e.activation(out=raw, in_=self.PS[:, bank * 512 + c0:bank * 512 + c0 + n],
                                              func=AF.Identity),
                reads=[("PS", bank)], writes=[("MODRAW", part)])
        bia = self.vec("adab", l * 96 + m0, n)
        self.op("dve", lambda e: e.tensor_tensor(out=mod, in0=raw, in1=bia, op=ALU.add),
                reads=[("MODRAW", part), ("VEC",)], writes=[("MOD", l, part)])
        DER = self.DER
        if part == "a":
            gn = self.vec("n1g", l * 16, 16)
            self.op("dve", lambda e: e.scalar_tensor_tensor(out=DER[:, l, 0, :], in0=self.MOD[:, l, 16:32], scalar=1.0,
                                                            in1=gn, op0=ALU.add, op1=ALU.mult),
                    reads=[("MOD", l, part), ("VEC",)], writes=[("DER", l, 0)])
        elif part == "b":
            if l == 0:
                ps = self.vec("pscale", 0, 16)
                self.op("dve", lambda e: e.scalar_tensor_tensor(out=DER[:, l, 2, :], in0=self.MOD[:, l, 32:48], scalar=1.0,
                                                                in1=ps, op0=ALU.add, op1=ALU.mult),
                        reads=[("MOD", l, part), ("VEC",)], writes=[("DER", l, 2)])
            else:
                self.op("dve", lambda e: e.tensor_scalar(out=DER[:, l, 2, :], in0=self.MOD[:, l, 32:48], scalar1=1.0,
                                                         scalar2=None, op0=ALU.add),
                        reads=[("MOD", l, part)], writes=[("DER", l, 2)])
        elif part == "c":
            gn = self.vec("n2g", l * 16, 16)
            self.op("dve", lambda e: e.scalar_tensor_tensor(out=DER[:, l, 3, :], in0=self.MOD[:, l, 64:80], scalar=1.0,
                                                            in1=gn, op0=ALU.add, op1=ALU.mult),
                    reads=[("MOD", l, part), ("VEC",)], writes=[("DER", l, 3)])
        else:
            self.op("dve", lambda e: e.tensor_scalar(out=DER[:, l, 5, :], in0=self.MOD[:, l, 80:96], scalar1=1.0,
                                                     scalar2=None, op0=ALU.add),
                    reads=[("MOD", l, part)], writes=[("DER", l, 5)])

    def coef(self, l, which):
        if which == "A1":
            return self.DER[:, l, 0, :], [("DER", l, 0)]
        if which == "B1":
            return self.MOD[:, l, 0:16], [("MOD", l, "a")]
        if which == "G1":
            return self.DER[:, l, 2, :], [("DER", l, 2)]
        if which == "A2":
            return self.DER[:, l, 3, :], [("DER", l, 3)]
        if which == "B2":
            return self.MOD[:, l, 48:64], [("MOD", l, "c")]
        if which == "G2":
            return self.DER[:, l, 5, :], [("DER", l, 5)]
        raise KeyError(which)

    def norm_stats(self):
        X, H = self.X, self.H
        for k in range(KC):
            if k % 2 == 0:
                self.op("act", lambda e, k=k: e.activation(out=H[:, k, :], in_=X[:, k, :], func=AF.Square),
                        reads=[("X", k)], writes=[("H", k)])
            else:
                self.op("dve", lambda e, k=k: e.tensor_tensor(out=H[:, k, :], in0=X[:, k, :], in1=X[:, k, :], op=ALU.mult),
                        reads=[("X", k)], writes=[("H", k)])
        steps = [(self.ONES[:, :], self.blkfn(H[:, k, :]), [("H", k), ("ONES",)]) for k in range(KC)]
        st, view = self.pe_group(steps)
        s = self.new_tmp()
        ts = self.tmp(s)
        out3 = self.TMP[:, s, :].rearrange("p (b c) -> p b c", c=BW)
        self.op("act", lambda e: e.activation(out=out3, in_=view, func=AF.Sqrt, bias=self.EPSV[:, 0:1]),
                reads=[("PS", st), ("EPSV",)], writes=[("TMP", s)])
        self.op("dve", lambda e: e.reciprocal(out=self.RSTD[:, :], in_=ts),
                reads=[("TMP", s)], writes=[("RSTD",)])

    def modulate(self, l, site, first_tile, to_h=True):
        A, ak = self.coef(l, "A%d" % site)
        Bv, bk = self.coef(l, "B%d" % site)
        X, H = self.X, self.H
        for k in range(KC):
            s = self.new_tmp()
            ts = self.tmp(s)
            self.op("dve", lambda e, k=k, ts=ts: e.tensor_tensor(out=ts, in0=X[:, k, :], in1=self.RSTD[:, :], op=ALU.mult),
                    reads=[("X", k), ("RSTD",)], writes=[("TMP", s)])
            self.op("act", lambda e, k=k, ts=ts: e.activation(out=H[:, k, :], in_=ts, func=AF.Identity,
                                                              bias=Bv[:, k:k + 1], scale=A[:, k:k + 1]),
                    reads=[("TMP", s)] + ak + bk, writes=[("H", k)])
            if first_tile:
                self.op("dve", lambda e, k=k: e.tensor_tensor(out=H[:, k, 0:HALO], in0=H[:, k, 0:HALO],
                                                              in1=self.MASKB[:, :], op=ALU.mult),
                        reads=[("H", k), ("MASKB",)], writes=[("H", k)])

    def resid_evac(self, st, view, l, which, m):
        G, gk = self.coef(l, which)
        s = self.evac(st, view, scale=G[:, m:m + 1], extra_reads=gk)
        ts = self.tmp(s)
        X = self.X
        self.op("dve", lambda e: e.tensor_tensor(out=X[:, m, :], in0=X[:, m, :], in1=ts, op=ALU.add),
                reads=[("TMP", s), ("X", m)], writes=[("X", m)])

    def conv3(self, src_s, w_ap3, b_ap):
        src = self.tmp(src_s)
        d = self.new_tmp()
        dst = self.tmp(d)
        self.op("dve", lambda e: e.tensor_scalar(out=dst, in0=src, scalar1=w_ap3[2], scalar2=b_ap,
                                                 op0=ALU.mult, op1=ALU.add),
                reads=[("TMP", src_s), ("VEC",)], writes=[("TMP", d)])
        self.op("dve", lambda e: e.scalar_tensor_tensor(out=dst[:, 1:T], in0=src[:, 0:T - 1], scalar=w_ap3[1],
                                                        in1=dst[:, 1:T], op0=ALU.mult, op1=ALU.add),
                reads=[("TMP", src_s), ("TMP", d), ("VEC",)], writes=[("TMP", d)])
        self.op("dve", lambda e: e.scalar_tensor_tensor(out=dst[:, 2:T], in0=src[:, 0:T - 2], scalar=w_ap3[0],
                                                        in1=dst[:, 2:T], op0=ALU.mult, op1=ALU.add),
                reads=[("TMP", src_s), ("TMP", d), ("VEC",)], writes=[("TMP", d)])
        return d

    def tref(self, ref):
        kind, u = ref
        if kind == "TMP":
            return self.TMP[:, u, :], [("TMP", u)]
        return self.PT[:, u, :], [("MID", (2 * u) // 8, (2 * u) % 8), ("MID", (2 * u + 1) // 8, (2 * u + 1) % 8)]

    def pool_front(self, first_tile):
        l = 0
        X, H = self.X, self.H
        A, ak = self.coef(l, "A1")
        Bv, bk = self.coef(l, "B1")
        self.norm_stats()
        dsets = [[("PT", 0), ("PT", 1), ("PT", 2)], [("PT", 3), ("PT", 4), ("PT", 5)]]
        psets = [[("PT", 6), ("PT", 7), ("TMP", 0)], [("TMP", 1), ("TMP", 2), ("TMP", 3)]]

        def stage1(k, ve, st):
            t, tk = self.tref(st[0])
            self.op(ve, lambda e: e.tensor_tensor(out=t, in0=X[:, k, :], in1=self.RSTD[:, :], op=ALU.mult),
                    reads=[("X", k), ("RSTD",)], writes=tk)
            self.op("act", lambda e: e.activation(out=t, in_=t, func=AF.Identity, bias=Bv[:, k:k + 1], scale=A[:, k:k + 1]),
                    reads=tk + ak + bk, writes=tk)
            if first_tile:
                self.op(ve, lambda e: e.tensor_tensor(out=t[:, 0:HALO], in0=t[:, 0:HALO], in1=self.vec("mask", 0, HALO),
                                                      op=ALU.mult), reads=tk + [("VEC",)], writes=tk)

        def stage2(k, ve, st):
            g = k // 4
            win = POOL_WINDOWS[g]
            cur, ck = self.tref(st[0])
            dd, i = 1, 0
            while dd < win:
                nx, nk = self.tref(st[1 + (i % 2)])
                self.op(ve, lambda e, cur=cur, nx=nx, dd=dd: e.tensor_tensor(out=nx[:, dd:T], in0=cur[:, dd:T],
                                                                              in1=cur[:, 0:T - dd], op=ALU.add),
                        reads=ck, writes=nk)
                cur, ck = nx, nk
                dd *= 2
                i += 1
            if first_tile:
                self.op(ve, lambda e, cur=cur: e.tensor_tensor(out=cur[:, HALO:HALO + 16], in0=cur[:, HALO:HALO + 16],
                                                               in1=self.vec("corr", g * 16, 16), op=ALU.mult),
                        reads=ck + [("VEC",)], writes=ck)
            return cur, ck

        def final(k, st, cur, ck):
            win = POOL_WINDOWS[k // 4]
            t, tk = self.tref(st[0])
            self.op("dve", lambda e: e.scalar_tensor_tensor(out=H[:, k, :], in0=cur, scalar=1.0 / win, in1=t,
                                                            op0=ALU.mult, op1=ALU.subtract),
                    reads=ck + tk, writes=[("H", k)])

        for g in range(4):
            c = [4 * g, 4 * g + 1, 4 * g + 2, 4 * g + 3]
            if g == 0:
                stage1(c[0], "dve", dsets[0])
                stage1(c[1], "dve", dsets[1])
            stage1(c[2], "dve", psets[0])
            stage1(c[3], "dve", psets[1])
            r0 = stage2(c[0], "dve", dsets[0])
            final(c[0], dsets[0], *r0)
            r1 = stage2(c[1], "dve", dsets[1])
            final(c[1], dsets[1], *r1)
            if g < 3:
                stage1(c[0] + 4, "dve", dsets[0])
                stage1(c[1] + 4, "dve", dsets[1])
            r2 = stage2(c[2], "dve", psets[0])
            r3 = stage2(c[3], "dve", psets[1])
            final(c[2], psets[0], *r2)
            final(c[3], psets[1], *r3)

    def pool_w_load(self):
        out = []
        for gp in range(2):
            srcs = []
            s = self.slot_ctr % NS
            self.slot_ctr += 1
            for gi in range(2):
                g = 2 * gp + gi
                dst = self.WR[:, s, gi * 2048:(gi + 1) * 2048].rearrange("p (k n) -> p k n", n=512)
                srcv = self.pool_w[0, g].rearrange("(k p) n -> p k n", p=128)
                self.op("pool", lambda e, dst=dst, srcv=srcv: e.dma_start(out=dst, in_=srcv),
                        writes=[("W", s)], dma_sem=self.SW[s])
                srcs.append(dst)
            out.append((s, srcs))
        return out

    def pool_back(self, pre=None):
        H = self.H
        if pre is None:
            pre = self.pool_w_load()
        for gp in range(2):
            s, srcs = pre[gp]
            for gi in range(2):
                g = 2 * gp + gi
                wv = srcs[gi]
                for mc in range(4):
                    steps = [(wv[:, kc, mc * 128:(mc + 1) * 128], self.blkfn(H[:, 4 * g + kc, :]),
                              [("H", 4 * g + kc), ("W", s)]) for kc in range(4)]
                    st, view = self.pe_group(steps)
                    self.resid_evac(st, view, 0, "G1", 4 * g + mc)

    def ffn(self, l, first_tile):
        H, MID = self.H, self.MID
        self.norm_stats()
        self.modulate(l, 2, first_tile)
        upw, dnw = self.up_w, self.down_w

        def up_block(b):
            nb = FFB[b]
            j0 = sum(FFB[:b])
            for jj in range(0, nb, 2):
                j = j0 + jj
                sg, wg = self.load_slot(upw[l, :, 128 * j:128 * j + 256], 16, 256)
                sv, wv = self.load_slot(upw[l, :, FF + 128 * j:FF + 128 * j + 256], 16, 256)
                for e2 in range(2):
                    jc = j + e2
                    steps = [(wg[:, k, e2 * 128:(e2 + 1) * 128], self.blkfn(H[:, k, :]), [("H", k), ("W", sg)])
                             for k in range(KC)]
                    stepsv = [(wv[:, k, e2 * 128:(e2 + 1) * 128], self.blkfn(H[:, k, :]), [("H", k), ("W", sv)])
                              for k in range(KC)]
                    il = (b == 0 and jj == 0 and e2 == 0)
                    if il:
                        (st, view), (stv_, viewv_) = self.pe_group_pair(steps, stepsv)
                    else:
                        st, view = self.pe_group(steps)
                    u = self.evac(st, view)
                    w3 = [self.vec("fcw", l * 264 + tap * 88 + jc, 1) for tap in range(3)]
                    ag = self.conv3(u, w3, self.vec("fcb", l * 88 + jc, 1))
                    tg = self.tmp(ag)
                    self.op("act", lambda e, tg=tg: e.activation(out=tg, in_=tg, func=AF.Silu),
                            reads=[("TMP", ag)], writes=[("TMP", ag)])
                    if il:
                        st, view = stv_, viewv_
                    else:
                        st, view = self.pe_group(stepsv)
                    u = self.evac(st, view)
                    w3 = [self.vec("fcw", l * 264 + tap * 88 + FC + jc, 1) for tap in range(3)]
                    av = self.conv3(u, w3, self.vec("fcb", l * 88 + FC + jc, 1))
                    tv = self.tmp(av)
                    mo = MID[:, b % 2, jj + e2, :]
                    self.op("dve", lambda e, tg=tg, tv=tv, mo=mo: e.tensor_tensor(out=mo, in0=tg, in1=tv, op=ALU.mult),
                            reads=[("TMP", ag), ("TMP", av)], writes=[("MID", b % 2, jj + e2)])
                    self.bg_step()

        def down_block(b):
            nb = FFB[b]
            j0 = sum(FFB[:b])
            for q in range(4):
                s, wd = self.load_slot(dnw[l, 128 * j0:128 * (j0 + nb), 512 * q:512 * q + 512], nb, 512)
                for mc in range(4):
                    m = 4 * q + mc
                    steps = [(wd[:, kc, mc * 128:(mc + 1) * 128], self.blkfn(MID[:, b % 2, kc, :]),
                              [("MID", b % 2, kc), ("W", s)]) for kc in range(nb)]
                    st, view = self.pe_group(steps)
                    self.resid_evac(st, view, l, "G2", m)
                    if mc % 2 == 1:
                        self.bg_step()

        nbk = len(FFB)
        up_block(0)
        for b in range(1, nbk):
            up_block(b)
            down_block(b - 1)
        down_block(nbk - 1)

    def sconv_layer(self, first_tile):
        l = 1
        H, BV = self.H, self.BV
        self.norm_stats()
        self.modulate(l, 1, first_tile)
        bw = self.bcx_w
        for j in range(0, KC, 2):
            sc, wc = self.load_slot(bw[0, :, D + 128 * j:D + 128 * j + 256], 16, 256)
            su, wu = self.load_slot(bw[0, :, 2 * D + 128 * j:2 * D + 128 * j + 256], 16, 256)
            sb, wb = self.load_slot(bw[0, :, 128 * j:128 * j + 256], 16, 256)
            for e2 in range(2):
                jc = j + e2
                steps = [(wc[:, k, e2 * 128:(e2 + 1) * 128], self.blkfn(H[:, k, :]), [("H", k), ("W", sc)]) for k in range(KC)]
                steps2 = [(wu[:, k, e2 * 128:(e2 + 1) * 128], self.blkfn(H[:, k, :]), [("H", k), ("W", su)]) for k in range(KC)]
                if j == 0 and e2 == 0:
                    (st, view), (st2, view2) = self.pe_group_pair(steps, steps2)
                    u1 = self.evac(st, view)
                    u2 = self.evac(st2, view2)
                else:
                    st, view = self.pe_group(steps)
                    u1 = self.evac(st, view)
                    st, view = self.pe_group(steps2)
                    u2 = self.evac(st, view)
                t1, t2 = self.tmp(u1), self.tmp(u2)
                self.op("dve", lambda e, t1=t1, t2=t2: e.tensor_tensor(out=t1, in0=t1, in1=t2, op=ALU.mult),
                        reads=[("TMP", u1), ("TMP", u2)], writes=[("TMP", u1)])
                w3 = [self.vec("scw", tap * 16 + jc, 1) for tap in range(3)]
                v = self.conv3(u1, w3, self.vec("scb", jc, 1))
                tv = self.tmp(v)
                steps = [(wb[:, k, e2 * 128:(e2 + 1) * 128], self.blkfn(H[:, k, :]), [("H", k), ("W", sb)]) for k in range(KC)]
                st, view = self.pe_group(steps)
                u3 = self.evac(st, view)
                t3 = self.tmp(u3)
                bo = BV[:, jc, :]
                self.op("dve", lambda e, t3=t3, tv=tv, bo=bo: e.tensor_tensor(out=bo, in0=t3, in1=tv, op=ALU.mult),
                        reads=[("TMP", u3), ("TMP", v)], writes=[("MID", jc // 8, jc % 8)])
        for mp in range(0, KC, 2):
            s, ws = self.load_slot(self.sout_w[0, :, 128 * mp:128 * mp + 256], 16, 256)
            for e2 in range(2):
                m = mp + e2
                steps = [(ws[:, k, e2 * 128:(e2 + 1) * 128], self.blkfn(BV[:, k, :]),
                          [("MID", k // 8, k % 8), ("W", s)]) for k in range(KC)]
                st, view = self.pe_group(steps)
                self.resid_evac(st, view, 1, "G1", m)

    def final_norm(self, i):
        X = self.X
        self.norm_stats()
        for k in range(KC):
            s = self.new_tmp()
            ts = self.tmp(s)
            self.op("dve", lambda e, k=k, ts=ts: e.tensor_tensor(out=ts, in0=X[:, k, :], in1=self.RSTD[:, :], op=ALU.mult),
                    reads=[("X", k), ("RSTD",)], writes=[("TMP", s)])
            fg = self.vec("fing", k, 1)
            self.op("act", lambda e, ts=ts, fg=fg: e.activation(out=ts, in_=ts, func=AF.Identity, scale=fg),
                    reads=[("TMP", s), ("VEC",)], writes=[("TMP", s)])
            dst = self.outT[k * 128:(k + 1) * 128, i * WV:(i + 1) * WV]
            self.op("sp", lambda e, ts=ts, dst=dst: e.dma_start(out=dst, in_=ts[:, HALO:T]),
                    reads=[("TMP", s)], dma_sem=self.SO[s])

    def generate(self):
        self.op("sp", lambda e: e.dma_start(out=self.VEC[:, :], in_=self.vecs), writes=[("VEC",)], dma_sem=self.SV)
        self.op("dve", lambda e: e.memset(self.ONES[:, :], 1.0 / D), writes=[("ONES",)])
        self.op("dve", lambda e: e.memset(self.EPSV[:, :], EPS), writes=[("EPSV",)])
        self.op("dve", lambda e: e.memset(self.TMP[:, :, :], 0.0), writes=[("TMP", t) for t in range(NTMP)])
        self.op("dve", lambda e: e.memset(self.PT[:, :, :], 0.0), writes=[("MID", a, b) for a in range(2) for b in range(8)])
        self.op("dve", lambda e: e.tensor_copy(out=self.CB[:, :], in_=self.vec("c", 0, 16)), reads=[("VEC",)], writes=[("CB",)])
        self.op("dve", lambda e: e.tensor_copy(out=self.MASKB[:, :], in_=self.vec("mask", 0, HALO)),
                reads=[("VEC",)], writes=[("MASKB",)])
        for i in range(self.NT):
            first = (i == 0)
            for k in range(KC):
                self.op("sp", lambda e, i=i, k=k: e.dma_start(
                    out=self.X[:, k, :], in_=self.xT[k * 128:(k + 1) * 128, i * WV:i * WV + T]),
                    writes=[("X", k)], dma_sem=self.SXK[k])
            if first:
                for j in range(16):
                    self.ada_block(0, j)
                self.ada_part(0, "a")
            pre = None if first else self.pool_w_load()
            self.pool_front(first)
            if first:
                for j in range(16, 24):
                    self.ada_block(0, j)
                self.ada_part(0, "b")
                for j in range(24, 40):
                    self.ada_block(0, j)
                self.ada_part(0, "c")
                for j in range(40, 48):
                    self.ada_block(0, j)
                self.ada_part(0, "d")
                self.bg_todo = [(1, m) for m in range(96)]
            self.pool_back(pre)
            self.ffn(0, first)
            if first:
                while self.bg_todo or self.bg_pending is not None or self.bg_cast is not None:
                    self.bg_step()
                for part in "abcd":
                    self.ada_part(1, part)
            self.sconv_layer(first)
            self.ffn(1, first)
            self.final_norm(i)
        finals = [(self.SO[s], self.dma_cnt.get(id(self.SO[s]), 0)) for s in range(NTMP)]

        def fin(e):
            ins = None
            for sem, val in finals:
                if val > 0:
                    ins = e.wait_ge(sem, val)
            return ins
        self.eng["sp"].ops.append(([], fin, None))


def build_program(NT):
    CW = NT * WV
    nc = bass.Bass("TRN2", target_bir_lowering=False)
    g = Gen(nc, NT)
    g.xT = nc.dram_tensor("xT", [D, CW + HALO], F32, kind="ExternalInput").ap()
    g.vecs = nc.dram_tensor("vecs", [128, NV], F32, kind="ExternalInput").ap()
    g.ada_w = nc.dram_tensor("ada_w", [2, D, 6 * D], F32, kind="ExternalInput").ap()
    g.pool_w = nc.dram_tensor("pool_w", [1, 4, 512, 512], F32, kind="ExternalInput").ap()
    g.bcx_w = nc.dram_tensor("bcx_w", [1, D, 3 * D], F32, kind="ExternalInput").ap()
    g.sout_w = nc.dram_tensor("sout_w", [1, D, D], F32, kind="ExternalInput").ap()
    g.up_w = nc.dram_tensor("up_w", [2, D, 2 * FF], F32, kind="ExternalInput").ap()
    g.down_w = nc.dram_tensor("down_w", [2, FF, D], F32, kind="ExternalInput").ap()
    g.outT = nc.dram_tensor("outT", [D, CW], F32, kind="ExternalOutput").ap()
    from contextlib import ExitStack
    with ExitStack() as es:
        def sb(name, shape, dt):
            return es.enter_context(nc.sbuf_tensor(name, shape, dt))
        g.X = sb("X", [128, KC, T], F32)
        g.H = sb("H", [128, KC, T], BF16)
        g.MIDF = sb("MID", [128, 16 * T], BF16)
        g.MID = g.MIDF[:, :].rearrange("p (a b t) -> p a b t", a=2, b=8)
        g.BV = g.MIDF[:, :].rearrange("p (c t) -> p c t", t=T)
        g.PT = g.MIDF[:, :].bitcast(F32).rearrange("p (u t) -> p u t", t=T)
        g.WR = sb("WR", [128, NS, 4096], BF16)
        g.TMP = sb("TMP", [128, NTMP, T], F32)
        g.RSTD = sb("RSTD", [128, T], F32)
        g.VEC = sb("VEC", [128, NV], F32)
        g.MODRAW = sb("MODRAW", [128, 96], F32)
        g.MOD = sb("MOD", [128, 2, 96], F32)
        g.DER = sb("DER", [128, 2, 6, 16], F32)
        g.ONES = sb("ONES", [128, 128], BF16)
        g.CB = sb("CB", [128, 16], BF16)
        g.MASKB = sb("MASKB", [128, HALO], BF16)
        g.EPSV = sb("EPSV", [128, 1], F32)
        g.ADAF = sb("ADAF", [128, 16, 128], F32)
        g.ADAB = sb("ADAB", [128, 16, 128], BF16)
        g.PS = es.enter_context(nc.psum_tensor("PS", [128, 4096], F32))
        for n in g.eng:
            g.eng[n].sem = es.enter_context(nc.semaphore("prog_" + n))
        g.SW = [es.enter_context(nc.semaphore("sw%d" % s)) for s in range(NS)]
        g.SO = [es.enter_context(nc.semaphore("so%d" % s)) for s in range(NTMP)]
        g.SXK = [es.enter_context(nc.semaphore("sx%d" % k)) for k in range(KC)]
        g.SV = es.enter_context(nc.semaphore("sv"))
        g.SA = es.enter_context(nc.semaphore("sa"))
        g.generate()
        with nc.Block() as block:
            @block.tensor
            def _(e):
                g.replay("pe", e)

            @block.scalar
            def _(e):
                g.replay("act", e)

            @block.vector
            def _(e):
                g.replay("dve", e)

            @block.gpsimd
            def _(e):
                g.replay("pool", e)

            @block.sync
            def _(e):
                g.replay("sp", e)
    return nc


def pack_vecs(c_row, q, ada_b, norm1_g, norm2_g, final_g, pool_scale, sconv_w, sconv_b, fconv_w, fconv_b):
    v = np.zeros((128, NV), np.float32)

    def put(name, off, arr):
        o = VOFF[name] + off
        a = np.asarray(arr, np.float32).reshape(-1, 128).T
        v[:, o:o + a.shape[1]] = a
    for l in range(2):
        put("adab", l * 96, ada_b[l])
        put("n1g", l * 16, norm1_g[l])
        put("n2g", l * 16, norm2_g[l])
        for tap in range(3):
            put("fcw", l * 264 + tap * 88, fconv_w[l, tap])
        put("fcb", l * 88, fconv_b[l])
    put("fing", 0, final_g)
    put("pscale", 0, pool_scale[0])
    for tap in range(3):
        put("scw", tap * 16, sconv_w[0, tap])
    put("scb", 0, sconv_b[0])
    put("c", 0, c_row)
    v[:, VOFF["mask"]:VOFF["mask"] + HALO] = 0.0 if q == 0 else 1.0
    for gi, win in enumerate(POOL_WINDOWS):
        o = VOFF["corr"] + gi * 16
        if q == 0:
            cnt = np.minimum(np.arange(16) + 1, win).astype(np.float32)
            v[:, o:o + 16] = (np.float32(win) / cnt)[None, :]
        else:
            v[:, o:o + 16] = 1.0
    return v


_PROG_CACHE = {}


def run(x, c, ada_w, ada_b, norm1_g, norm2_g, pool_w, pool_scale, bcx_w, sconv_w, sconv_b, sout_w, up_w,
        fconv_w, fconv_b, down_w, final_g, trace=False):
    x = np.asarray(x, np.float32)
    B, S, _ = x.shape
    QN = 8 // B
    CW = S // QN
    NT = CW // WV
    assert NT * WV * QN == S
    if NT not in _PROG_CACHE:
        _PROG_CACHE[NT] = build_program(NT)
    nc = _PROG_CACHE[NT]
    f = lambda a: np.ascontiguousarray(np.asarray(a, np.float32))
    shared = {"ada_w": f(ada_w), "pool_w": f(pool_w), "bcx_w": f(bcx_w), "sout_w": f(sout_w),
              "up_w": f(up_w), "down_w": f(down_w)}
    in_maps = []
    for core in range(8):
        b, q = core // QN, core % QN
        xs = np.zeros((D, CW + HALO), np.float32)
        lo = q * CW - HALO
        if lo < 0:
            xs[:, HALO:] = x[b, 0:CW, :].T
        else:
            xs[:, :] = x[b, lo:lo + CW + HALO, :].T
        m = dict(shared)
        m["xT"] = xs
        m["vecs"] = pack_vecs(np.asarray(c, np.float32)[b], q, np.asarray(ada_b), np.asarray(norm1_g), np.asarray(norm2_g),
                              np.asarray(final_g), np.asarray(pool_scale), np.asarray(sconv_w), np.asarray(sconv_b),
                              np.asarray(fconv_w), np.asarray(fconv_b))
        in_maps.append(m)
    res = run_bass_kernel_spmd(nc, in_maps, core_ids=list(range(8)), **({"trace": True} if trace else {}))
    out = np.empty((B, S, D), np.float32)
    for core in range(8):
        b, q = core // QN, core % QN
        out[b, q * CW:(q + 1) * CW, :] = res.results[core]["outT"].T
    return out, res


def kernel(**inputs):
    out, _ = run(**inputs)
    return out
```
